# Optimizing a Trainium2 kernel written in Bass

```python
import jax, jax.numpy as jnp
from jax import lax
import numpy as np

D_MODEL = 4096
BATCH = 4
SEQ = 2048
DEPTH = 1

A_HEAD_DIM = 128
A_WIDTH = D_MODEL // 2
A_HEADS = A_WIDTH // A_HEAD_DIM
MOBA_BLOCK = 256
MOBA_TOPK = 3
MOBA_Q_CHUNK = 16
B_HEAD_DIM = 64
B_WIDTH = D_MODEL // 2
B_Q_HEADS = B_WIDTH // B_HEAD_DIM
B_GROUP = 8
B_KV_HEADS = B_Q_HEADS // B_GROUP
B_KV_WIDTH = B_KV_HEADS * B_HEAD_DIM
WINDOW = 128
D_FF = 4 * D_MODEL
PLE_DIM = 256
ROPE_THETA = 10000.0
LN_EPS = 1e-5
DEEPNORM_ALPHA = (2.0 * DEPTH) ** 0.25
DEEPNORM_BETA = (8.0 * DEPTH) ** -0.25
IN_SIZES = (A_WIDTH, A_WIDTH, A_WIDTH, B_WIDTH, B_KV_WIDTH, B_KV_WIDTH, D_MODEL, D_MODEL)
IN_SPLITS = tuple(int(s) for s in np.cumsum(IN_SIZES)[:-1])
IN_TOTAL = int(sum(IN_SIZES))

kernel_name = 'hybrid_moba_swa_sink_gated_deepnorm'


def rope(x, pos):
    half = x.shape[-1] // 2
    inv_freq = ROPE_THETA ** (-jnp.arange(half, dtype=jnp.float32) / half)
    ang = pos.astype(jnp.float32)[:, None] * inv_freq[None, :]
    cos = jnp.cos(ang).astype(x.dtype)
    sin = jnp.sin(ang).astype(x.dtype)
    x1, x2 = x[..., :half], x[..., half:]
    return jnp.concatenate([x1 * cos - x2 * sin, x1 * sin + x2 * cos], axis=-1)


def layer_norm(x, g, b):
    xf = x.astype(jnp.float32)
    mu = jnp.mean(xf, axis=-1, keepdims=True)
    var = jnp.mean(jnp.square(xf - mu), axis=-1, keepdims=True)
    y = (xf - mu) * lax.rsqrt(var + LN_EPS) * g.astype(jnp.float32) + b.astype(jnp.float32)
    return y.astype(x.dtype)


def moba_attention(q, k, v):
    bsz, n_heads, seq, hd = q.shape
    n_blocks = -(-seq // MOBA_BLOCK)
    pad = n_blocks * MOBA_BLOCK - seq
    k_blocks = jnp.pad(k, ((0, 0), (0, 0), (0, pad), (0, 0))).reshape(bsz, n_heads, n_blocks, MOBA_BLOCK, hd)
    v_blocks = jnp.pad(v, ((0, 0), (0, 0), (0, pad), (0, 0))).reshape(bsz, n_heads, n_blocks, MOBA_BLOCK, hd)
    k_mean = jnp.mean(k_blocks.astype(jnp.float32), axis=3)
    top_k = min(MOBA_TOPK, max(n_blocks - 1, 1))
    scale = hd ** -0.5
    b_idx = jnp.arange(bsz)[:, None, None]
    h_idx = jnp.arange(n_heads)[None, :, None]
    blk_ids = jnp.arange(n_blocks)
    key_offsets = jnp.arange(MOBA_BLOCK)

    def attend_chunk(c):
        q0 = c * MOBA_Q_CHUNK
        qc = lax.dynamic_slice_in_dim(q, q0, MOBA_Q_CHUNK, axis=2)
        blk = q0 // MOBA_BLOCK
        q_pos = q0 + jnp.arange(MOBA_Q_CHUNK)
        gate = jnp.einsum('bhqd,bhnd->bhqn', qc.astype(jnp.float32), k_mean)
        gate = jnp.where(blk_ids < blk, gate, -jnp.inf)
        _, sel = lax.top_k(gate, top_k)
        valid = sel < blk
        flat = sel.reshape(bsz, n_heads, MOBA_Q_CHUNK * top_k)
        k_sel = k_blocks[b_idx, h_idx, flat].reshape(bsz, n_heads, MOBA_Q_CHUNK, top_k, MOBA_BLOCK, hd)
        v_sel = v_blocks[b_idx, h_idx, flat].reshape(bsz, n_heads, MOBA_Q_CHUNK, top_k, MOBA_BLOCK, hd)
        s_sel = jnp.einsum('bhqd,bhqkjd->bhqkj', qc, k_sel, preferred_element_type=jnp.float32) * scale
        s_sel = jnp.where(valid[..., None], s_sel, -jnp.inf)
        k_own = lax.dynamic_index_in_dim(k_blocks, blk, axis=2, keepdims=False)
        v_own = lax.dynamic_index_in_dim(v_blocks, blk, axis=2, keepdims=False)
        s_own = jnp.einsum('bhqd,bhjd->bhqj', qc, k_own, preferred_element_type=jnp.float32) * scale
        k_pos = blk * MOBA_BLOCK + key_offsets
        s_own = jnp.where(k_pos[None, :] <= q_pos[:, None], s_own, -jnp.inf)
        s = jnp.concatenate([s_sel.reshape(bsz, n_heads, MOBA_Q_CHUNK, top_k * MOBA_BLOCK), s_own], axis=-1)
        prob = jax.nn.softmax(s, axis=-1)
        p_sel = prob[..., :top_k * MOBA_BLOCK].reshape(bsz, n_heads, MOBA_Q_CHUNK, top_k, MOBA_BLOCK).astype(v.dtype)
        p_own = prob[..., top_k * MOBA_BLOCK:].astype(v.dtype)
        return (jnp.einsum('bhqkj,bhqkjd->bhqd', p_sel, v_sel)
                + jnp.einsum('bhqj,bhjd->bhqd', p_own, v_own))

    out = lax.map(attend_chunk, jnp.arange(seq // MOBA_Q_CHUNK))
    return out.transpose(1, 2, 0, 3, 4).reshape(bsz, n_heads, seq, hd)


def swa_sink_attention(q, k, v, sinks):
    bsz, n_kv, grp, seq, hd = q.shape
    n_qb = seq // WINDOW
    qb = q.reshape(bsz, n_kv, grp, n_qb, WINDOW, hd)

    def band(t):
        tp = jnp.pad(t, ((0, 0), (0, 0), (WINDOW, 0), (0, 0)))
        prev = tp[:, :, :seq].reshape(bsz, n_kv, n_qb, WINDOW, hd)
        cur = tp[:, :, WINDOW:].reshape(bsz, n_kv, n_qb, WINDOW, hd)
        return jnp.concatenate([prev, cur], axis=3)

    k_band, v_band = band(k), band(v)
    s = jnp.einsum('bkgnqd,bknjd->bkgnqj', qb, k_band, preferred_element_type=jnp.float32) * (hd ** -0.5)
    n = jnp.arange(n_qb)[:, None, None]
    q_abs = n * WINDOW + jnp.arange(WINDOW)[None, :, None]
    k_abs = n * WINDOW + jnp.arange(2 * WINDOW)[None, None, :] - WINDOW
    mask = (k_abs >= 0) & (k_abs <= q_abs) & (q_abs - k_abs < WINDOW)
    s = jnp.where(mask, s, -jnp.inf)
    sink_col = jnp.broadcast_to(sinks.astype(jnp.float32)[None, :, :, None, None, None], s.shape[:-1] + (1,))
    prob = jax.nn.softmax(jnp.concatenate([s, sink_col], axis=-1), axis=-1)[..., :-1]
    out = jnp.einsum('bkgnqj,bknjd->bkgnqd', prob.astype(v.dtype), v_band)
    return out.reshape(bsz, n_kv, grp, seq, hd)


def setup_inputs(seed: int = 0) -> dict:
    key = jax.random.key(seed)
    ks = jax.random.split(key, 16)
    f32 = jnp.float32
    nrm = lambda k, shape, s: jax.random.normal(k, shape, f32) * s
    col_scale = jnp.concatenate([
        jnp.ones((2 * A_WIDTH,), f32), jnp.full((A_WIDTH,), DEEPNORM_BETA, f32),
        jnp.ones((B_WIDTH + B_KV_WIDTH,), f32), jnp.full((B_KV_WIDTH,), DEEPNORM_BETA, f32),
        jnp.ones((2 * D_MODEL,), f32)])
    return {
        'x': nrm(ks[0], (BATCH, SEQ, D_MODEL), 1.0),
        'p': nrm(ks[1], (DEPTH, BATCH, SEQ, PLE_DIM), 1.0),
        'w_in': nrm(ks[2], (DEPTH, D_MODEL, IN_TOTAL), D_MODEL ** -0.5) * col_scale,
        'b_gate': nrm(ks[3], (DEPTH, 2, D_MODEL), 0.02),
        'sinks': nrm(ks[4], (DEPTH, B_Q_HEADS), 0.5),
        'w_up_a': nrm(ks[5], (DEPTH, A_WIDTH, D_MODEL), DEEPNORM_BETA * A_WIDTH ** -0.5),
        'w_up_b': nrm(ks[6], (DEPTH, B_WIDTH, D_MODEL), DEEPNORM_BETA * B_WIDTH ** -0.5),
        'w_o': nrm(ks[7], (DEPTH, D_MODEL, D_MODEL), DEEPNORM_BETA * D_MODEL ** -0.5),
        'ln1_g': 1.0 + nrm(ks[8], (DEPTH, D_MODEL), 0.02),
        'ln1_b': nrm(ks[9], (DEPTH, D_MODEL), 0.02),
        'w_ff_up': nrm(ks[10], (DEPTH, D_MODEL, D_FF), D_MODEL ** -0.5),
        'w_ff_down': nrm(ks[11], (DEPTH, D_FF, D_MODEL), DEEPNORM_BETA * D_FF ** -0.5),
        'ln2_g': 1.0 + nrm(ks[12], (DEPTH, D_MODEL), 0.02),
        'ln2_b': nrm(ks[13], (DEPTH, D_MODEL), 0.02),
        'w_ple': nrm(ks[14], (DEPTH, PLE_DIM, D_MODEL), PLE_DIM ** -0.5),
        'w_ple_gate': nrm(ks[15], (DEPTH, D_MODEL, D_MODEL), D_MODEL ** -0.5),
    }


def reference(x, p, w_in, b_gate, sinks, w_up_a, w_up_b, w_o, ln1_g, ln1_b,
              w_ff_up, w_ff_down, ln2_g, ln2_b, w_ple, w_ple_gate):
    bsz, seq, _ = x.shape
    pos = jnp.arange(seq)
    h = x
    for i in range(DEPTH):
        z = jnp.einsum('bsd,df->bsf', h, w_in[i])
        qa, ka, va, qb, kb, vb, ga, gb = jnp.split(z, IN_SPLITS, axis=-1)
        to_heads_a = lambda t: t.reshape(bsz, seq, A_HEADS, A_HEAD_DIM).transpose(0, 2, 1, 3)
        qa_h, ka_h, va_h = rope(to_heads_a(qa), pos), rope(to_heads_a(ka), pos), to_heads_a(va)
        ya = moba_attention(qa_h, ka_h, va_h).transpose(0, 2, 1, 3).reshape(bsz, seq, A_WIDTH)
        qb_h = rope(qb.reshape(bsz, seq, B_KV_HEADS, B_GROUP, B_HEAD_DIM).transpose(0, 2, 3, 1, 4), pos)
        to_heads_b = lambda t: t.reshape(bsz, seq, B_KV_HEADS, B_HEAD_DIM).transpose(0, 2, 1, 3)
        kb_h, vb_h = rope(to_heads_b(kb), pos), to_heads_b(vb)
        yb = swa_sink_attention(qb_h, kb_h, vb_h, sinks[i].reshape(B_KV_HEADS, B_GROUP))
        yb = yb.transpose(0, 3, 1, 2, 4).reshape(bsz, seq, B_WIDTH)
        gate_a = jax.nn.sigmoid(ga + b_gate[i, 0])
        gate_b = jax.nn.sigmoid(gb + b_gate[i, 1])
        merged = (gate_a * jnp.einsum('bsc,cd->bsd', ya, w_up_a[i])
                  + gate_b * jnp.einsum('bsc,cd->bsd', yb, w_up_b[i]))
        mix = jnp.einsum('bsd,de->bse', merged, w_o[i])
        h = layer_norm(DEEPNORM_ALPHA * h + mix, ln1_g[i], ln1_b[i])
        u = jnp.square(jax.nn.relu(jnp.einsum('bsd,df->bsf', h, w_ff_up[i])))
        ff = jnp.einsum('bsf,fd->bsd', u, w_ff_down[i])
        h = layer_norm(DEEPNORM_ALPHA * h + ff, ln2_g[i], ln2_b[i])
        ple = jnp.einsum('bsp,pd->bsd', p[i], w_ple[i])
        h = h + jax.nn.sigmoid(jnp.einsum('bsd,de->bse', h, w_ple_gate[i])) * ple
    return h
```

```python
import numpy as np
import concourse.bass as bass
import concourse.mybir as mybir
from concourse.bass_utils import run_bass_kernel_spmd

F32 = mybir.dt.float32
BF16 = mybir.dt.bfloat16
AF = mybir.ActivationFunctionType
ALU = mybir.AluOpType
AX = mybir.AxisListType

NEG = -30000.0
ROPE_THETA = 10000.0
LN_EPS = 1e-5


class Cfg:
    def __init__(self, D=4096, debug=False, ring=4):
        self.D = D
        self.KC = D // 128
        self.DC = D // 128
        self.AW = D // 2
        self.AH = self.AW // 128
        self.BW = D // 2
        self.BQH = self.BW // 64
        self.BC = self.BW // 128
        self.NKV = self.BQH // 8
        self.CPG = self.BC // self.NKV
        self.KVW = self.NKV * 64
        self.FF = 4 * D
        self.FC = self.FF // 128
        self.FGS = 8
        self.FG = self.FC // self.FGS
        self.PLE = 256
        self.SEQ = 2048
        self.T = 1024
        self.TG = 512
        self.NTG = 2
        self.KPS = 16
        self.SL = self.KPS * 128
        self.ring = ring
        self.NPO = min(2, self.DC)
        self.alpha = 2.0 ** 0.25
        self.debug = debug
        c = 0
        self.v_bga = c; c += self.DC
        self.v_bgb = c; c += self.DC
        self.v_l1g = c; c += self.DC
        self.v_l1b = c; c += self.DC
        self.v_l2g = c; c += self.DC
        self.v_l2b = c; c += self.DC
        self.v_sink = c; c += self.BC
        self.v_slot = c; c += 2
        self.v_eps = c; c += 1
        self.NV = c
        self.NG = 128
        c = 0
        self.c_tri = c; c += 128
        self.c_tri2 = c; c += 128
        self.c_tri2f = c; c += 128
        self.c_id = c; c += 128
        self.c_ones = c; c += 128
        self.c_on0 = c; c += 128
        self.c_on1 = c; c += 128
        self.c_E = c; c += 8 * 128
        self.c_negm = c; c += 128
        self.NCB = c


class Tile:
    __slots__ = ("name", "w", "r")

    def __init__(self, name, init=()):
        self.name = name
        self.w = None
        self.r = list(init)


class Eng:
    def __init__(self, nc, eng, name, is_pe=False):
        self.e = eng
        self.name = name
        self.sem = nc.alloc_semaphore(name="s_" + name)
        self.n = 0
        self.waited = {}
        self.is_pe = is_pe

    def wait(self, ev):
        sem, val, _ = ev
        k = id(sem)
        if self.waited.get(k, 0) >= val:
            return
        self.e.wait_ge(sem, val)
        self.waited[k] = val

    def last(self):
        return (self.sem, self.n, self.name)


class DSem:
    def __init__(self, nc, name):
        self.sem = nc.alloc_semaphore(name="d_" + name)
        self.val = 0
        self.name = name


class Sched:
    def __init__(self, nc):
        self.nc = nc
        self.pe = Eng(nc, nc.tensor, "pe", True)
        self.act = Eng(nc, nc.scalar, "act")
        self.dve = Eng(nc, nc.vector, "dve")
        self.pool = Eng(nc, nc.gpsimd, "pool")
        self.sp = Eng(nc, nc.sync, "sp")
        self.nops = 0

    def _deps(self, E, reads, writes, extra):
        for ev in extra:
            if ev is not None and ev[1] > 0:
                E.wait(ev)
        for t in reads:
            if t.w is not None and not (E.is_pe and t.w[2] == "pe"):
                E.wait(t.w)
        for t in writes:
            if t.w is not None and not (E.is_pe and t.w[2] == "pe"):
                E.wait(t.w)
            for ev in t.r:
                if not (E.is_pe and ev[2] == "pe"):
                    E.wait(ev)

    @staticmethod
    def _upd(ev, reads, writes):
        for t in writes:
            t.w = ev
            t.r = []
        for t in reads:
            t.r = [x for x in t.r if x[0] is not ev[0]] + [ev]

    def op(self, E, fn, reads=(), writes=(), sig=True, extra=()):
        assert sig or E.is_pe
        self._deps(E, reads, writes, extra)
        ins = fn(E.e)
        self.nops += 1
        if sig:
            E.n += 1
            ins.then_inc(E.sem, 1)
            ev = (E.sem, E.n, E.name)
        else:
            ev = (E.sem, E.n + 1, E.name)
        self._upd(ev, reads, writes)
        return ev

    def dma(self, Q, ds, fns, reads=(), writes=(), extra=()):
        self._deps(Q, reads, writes, extra)
        for fn in fns:
            fn(Q.e).then_inc(ds.sem, 16)
            ds.val += 16
            self.nops += 1
        ev = (ds.sem, ds.val, "dma_" + ds.name)
        self._upd(ev, reads, writes)
        return ev

    def barrier(self):
        evs = [self.pe.last(), self.act.last(), self.dve.last()]
        for E in (self.pe, self.act, self.dve):
            for ev in evs:
                if ev[1] > 0 and ev[2] != E.name:
                    E.wait(ev)
        return [ev for ev in evs if ev[1] > 0]


def _tiles_from_cols(W, col_lists):
    K = W.shape[0]
    kc = K // 128
    out = np.empty((len(col_lists), 128, kc * 128), np.float32)
    for i, cols in enumerate(col_lists):
        blk = W[:, cols]
        out[i] = blk.reshape(kc, 128, 128).transpose(1, 0, 2).reshape(128, kc * 128)
    return out


def _tiles_contig(W, c0, n):
    K = W.shape[0]
    kc = K // 128
    blk = W[:, c0:c0 + n * 128].reshape(kc, 128, n, 128)
    return np.ascontiguousarray(blk.transpose(2, 1, 0, 3)).reshape(n, 128, kc * 128)


def prep_shared(cfg, inp):
    D, AW, BW, KVW = cfg.D, cfg.AW, cfg.BW, cfg.KVW
    w_in = np.asarray(inp["w_in"][0], np.float32)
    o_qa, o_ka, o_va = 0, AW, 2 * AW
    o_qb = 3 * AW
    o_kb = o_qb + BW
    o_vb = o_kb + KVW
    o_ga = o_vb + KVW
    o_gb = o_ga + D
    sh = {}
    sh["wqa"] = _tiles_contig(w_in, o_qa, cfg.AH)
    sh["wka"] = _tiles_contig(w_in, o_ka, cfg.AH)
    sh["wva"] = _tiles_contig(w_in, o_va, cfg.AH)
    qb_cols = []
    for c in range(cfg.BC):
        h0, h1 = o_qb + (2 * c) * 64, o_qb + (2 * c + 1) * 64
        qb_cols.append(np.concatenate([np.arange(h0, h0 + 32), np.arange(h1, h1 + 32),
                                       np.arange(h0 + 32, h0 + 64), np.arange(h1 + 32, h1 + 64)]))
    sh["wqb"] = _tiles_from_cols(w_in, qb_cols)
    kb_cols, vb_cols = [], []
    for g in range(cfg.NKV):
        k0 = o_kb + g * 64
        kb_cols.append(np.concatenate([np.arange(k0, k0 + 32), np.arange(k0, k0 + 32),
                                       np.arange(k0 + 32, k0 + 64), np.arange(k0 + 32, k0 + 64)]))
        v0 = o_vb + g * 64
        vb_cols.append(np.concatenate([np.arange(v0, v0 + 64), np.arange(v0, v0 + 64)]))
    sh["wkb"] = _tiles_from_cols(w_in, kb_cols)
    sh["wvb"] = _tiles_from_cols(w_in, vb_cols)
    sh["wga"] = _tiles_contig(w_in, o_ga, cfg.DC)
    sh["wgb"] = _tiles_contig(w_in, o_gb, cfg.DC)
    sh["wupa"] = _tiles_contig(np.asarray(inp["w_up_a"][0], np.float32), 0, cfg.DC)
    sh["wupb"] = _tiles_contig(np.asarray(inp["w_up_b"][0], np.float32), 0, cfg.DC)
    sh["wo"] = _tiles_contig(np.asarray(inp["w_o"][0], np.float32), 0, cfg.DC)
    sh["wffu"] = _tiles_contig(np.asarray(inp["w_ff_up"][0], np.float32), 0, cfg.FC)
    wd = np.asarray(inp["w_ff_down"][0], np.float32).reshape(cfg.FG, cfg.FGS, 128, cfg.DC // 2, 2, 128)
    sh["wffd"] = np.ascontiguousarray(wd.transpose(0, 3, 2, 4, 1, 5)).reshape(
        cfg.FG * (cfg.DC // 2), 128, 2 * cfg.FGS * 128)
    sh["wpg"] = _tiles_contig(np.asarray(inp["w_ple_gate"][0], np.float32), 0, cfg.DC)
    wp = np.asarray(inp["w_ple"][0], np.float32).reshape(2, 128, cfg.DC // cfg.NPO, cfg.NPO, 128)
    sh["wple"] = np.ascontiguousarray(wp.transpose(2, 1, 3, 0, 4)).reshape(
        cfg.DC // cfg.NPO, 128, cfg.NPO * 2 * 128)
    vec = np.zeros((128, cfg.NV), np.float32)
    pm = lambda v: np.asarray(v, np.float32).reshape(cfg.DC, 128).T
    vec[:, cfg.v_bga:cfg.v_bga + cfg.DC] = pm(inp["b_gate"][0, 0])
    vec[:, cfg.v_bgb:cfg.v_bgb + cfg.DC] = pm(inp["b_gate"][0, 1])
    vec[:, cfg.v_l1g:cfg.v_l1g + cfg.DC] = pm(inp["ln1_g"][0])
    vec[:, cfg.v_l1b:cfg.v_l1b + cfg.DC] = pm(inp["ln1_b"][0])
    vec[:, cfg.v_l2g:cfg.v_l2g + cfg.DC] = pm(inp["ln2_g"][0])
    vec[:, cfg.v_l2b:cfg.v_l2b + cfg.DC] = pm(inp["ln2_b"][0])
    sinks = np.asarray(inp["sinks"][0], np.float32)
    for c in range(cfg.BC):
        vec[0:64, cfg.v_sink + c] = sinks[2 * c]
        vec[64:128, cfg.v_sink + c] = sinks[2 * c + 1]
    p = np.arange(128)
    vec[:, cfg.v_slot + 0] = ((p % 64) < 32).astype(np.float32)
    vec[:, cfg.v_slot + 1] = ((p % 64) >= 32).astype(np.float32)
    vec[:, cfg.v_eps] = LN_EPS
    sh["vecs"] = vec
    cb = np.zeros((128, cfg.NCB), np.float32)
    k = np.arange(128)[:, None]
    q = np.arange(128)[None, :]
    cb[:, cfg.c_tri:cfg.c_tri + 128] = np.where(k <= q, 0.0, NEG)
    cb[:, cfg.c_tri2:cfg.c_tri2 + 128] = np.where(k > q, 0.0, NEG)
    cb[:, cfg.c_id:cfg.c_id + 128] = np.eye(128)
    cb[:, cfg.c_ones:cfg.c_ones + 128] = 1.0
    cb[:, cfg.c_on0:cfg.c_on0 + 64] = 1.0
    cb[:, cfg.c_on1 + 64:cfg.c_on1 + 128] = 1.0
    for n in range(8):
        cb[n, cfg.c_E + n * 128:cfg.c_E + (n + 1) * 128] = 1.0
    cb[:, cfg.c_negm:cfg.c_negm + 128] = NEG
    sh["cb"] = cb
    sh["ones32"] = np.ones((128, 128), np.float32)
    return sh


def prep_core(cfg, inp, sh, core):
    b, half = core // 2, core % 2
    T = cfg.T
    x = inp["x"]
    m = dict(sh)
    m["xt_own"] = np.ascontiguousarray(np.asarray(x[b, half * T:(half + 1) * T, :], np.float32).T)
    if half == 1:
        m["xt_prev"] = np.ascontiguousarray(np.asarray(x[b, 0:T, :], np.float32).T)
    else:
        m["xt_prev"] = np.zeros((cfg.D, T), np.float32)
    m["pt"] = np.ascontiguousarray(np.asarray(inp["p"][0, b, half * T:(half + 1) * T, :], np.float32).T)
    pos = (half * T - T + np.arange(2 * T)).astype(np.float64)
    pidx = np.arange(128)
    fa = ROPE_THETA ** (-(pidx % 64).astype(np.float64) / 64.0)
    fb = ROPE_THETA ** (-(pidx % 32).astype(np.float64) / 32.0)
    sgn = np.where(pidx < 64, 1.0, -1.0)[:, None]
    def tabs(f):
        ang = (pos.astype(np.float32)[None, :] * f.astype(np.float32)[:, None]).astype(np.float32)
        return np.cos(ang).astype(np.float32), (np.sin(ang) * sgn).astype(np.float32)
    ca, sa = tabs(fa)
    cbt, sbt = tabs(fb)
    m["ropeAp"] = np.ascontiguousarray(np.concatenate([ca[:, :T], sa[:, :T]], axis=1))
    m["ropeAo"] = np.ascontiguousarray(np.concatenate([ca[:, T:], sa[:, T:]], axis=1))
    m["ropeBo"] = np.ascontiguousarray(np.concatenate([cbt[:, T:], sbt[:, T:]], axis=1))
    m["ropeBp"] = np.ascontiguousarray(np.concatenate([cbt[:, T - 128:T], sbt[:, T - 128:T]], axis=1))
    gb = np.zeros((128, cfg.NG), np.float32)
    for qt in range(8):
        j = qt // 2
        for n in range(8):
            valid = (n < 4 + j) and (half == 1 or n >= 4)
            gb[:, qt * 8 + n] = 0.0 if valid else -1e30
            gb[:, 64 + qt * 8 + n] = 0.0 if n == 4 + j else NEG
    m["gbias"] = gb
    cbm = sh["cb"].copy()
    k = np.arange(128)[:, None]
    q = np.arange(128)[None, :]
    cbm[:, cfg.c_tri2f:cfg.c_tri2f + 128] = np.where(k > q, 0.0, NEG) if half == 1 else NEG
    m["cb"] = cbm
    return m


def build(cfg):
    nc = bass.Bass("TRN2", target_bir_lowering=False)
    D, KC, DC, AH, BC, NKV, T, TG = cfg.D, cfg.KC, cfg.DC, cfg.AH, cfg.BC, cfg.NKV, cfg.T, cfg.TG
    KPS, SL = cfg.KPS, cfg.SL

    def din(name, shape):
        return nc.dram_tensor(name, list(shape), F32, kind="ExternalInput").ap()

    d_xo = din("xt_own", [D, T])
    d_xp = din("xt_prev", [D, T])
    d_pt = din("pt", [cfg.PLE, T])
    d_ropeAp = din("ropeAp", [128, 2 * T])
    d_ropeAo = din("ropeAo", [128, 2 * T])
    d_ropeBo = din("ropeBo", [128, 2 * T])
    d_ropeBp = din("ropeBp", [128, 256])
    d_gbias = din("gbias", [128, cfg.NG])
    d_cb = din("cb", [128, cfg.NCB])
    d_vecs = din("vecs", [128, cfg.NV])
    d_ones32 = din("ones32", [128, 128])
    d_w = {}
    for nm, nt, L in (("wqa", AH, KC * 128), ("wka", AH, KC * 128), ("wva", AH, KC * 128),
                      ("wqb", BC, KC * 128), ("wkb", NKV, KC * 128), ("wvb", NKV, KC * 128),
                      ("wga", DC, KC * 128), ("wgb", DC, KC * 128),
                      ("wupa", DC, AH * 128), ("wupb", DC, BC * 128), ("wo", DC, DC * 128),
                      ("wffu", cfg.FC, KC * 128), ("wffd", cfg.FG * (DC // 2), 2 * cfg.FGS * 128),
                      ("wpg", DC, KC * 128), ("wple", DC // cfg.NPO, cfg.NPO * 2 * 128)):
        d_w[nm] = din(nm, [nt, 128, L])
    d_out = nc.dram_tensor("out_t", [D, T], F32, kind="ExternalOutput").ap()
    dbg = {}
    if cfg.debug:
        for nm, shp in (("dbg_ya", [cfg.AW, T]), ("dbg_yb", [cfg.BW, T]), ("dbg_mg", [D, T]),
                        ("dbg_h1", [D, T]), ("dbg_h2", [D, T])):
            dbg[nm] = nc.dram_tensor(nm, shp, F32, kind="ExternalOutput").ap()

    S = Sched(nc)
    PE, ACT, DVE, POOL, SP = S.pe, S.act, S.dve, S.pool, S.sp

    A_N = cfg.NTG * KC * TG
    B_N = max((2 * AH + 1) * 1024, DC * 1024)
    C_N = max(DC * 512, 16384)
    P_N = 6144
    from contextlib import ExitStack
    es = ExitStack()
    ring_t = es.enter_context(nc.sbuf_tensor("ring", [128, cfg.ring * SL], BF16))
    A_t = es.enter_context(nc.sbuf_tensor("arA", [128, A_N], BF16))
    B_t = es.enter_context(nc.sbuf_tensor("arB", [128, B_N], BF16))
    C_t = es.enter_context(nc.sbuf_tensor("arC", [128, C_N], BF16))
    P_t = es.enter_context(nc.sbuf_tensor("arP", [128, P_N], BF16))
    cb_t = es.enter_context(nc.sbuf_tensor("cbb", [128, cfg.NCB], BF16))
    vec_t = es.enter_context(nc.sbuf_tensor("vecs_sb", [128, cfg.NV + 2 * DC + BC], F32))
    gb_t = es.enter_context(nc.sbuf_tensor("gbias_sb", [128, cfg.NG], F32))
    one32_t = es.enter_context(nc.sbuf_tensor("ones32_sb", [128, 128], F32))
    pt_t = es.enter_context(nc.sbuf_tensor("ptb", [128, 2 * T], BF16))
    kvb_t = es.enter_context(nc.sbuf_tensor("kvbprev", [128, 2 * NKV * 128], BF16))
    ple_t = es.enter_context(nc.sbuf_tensor("pleslot", [128, 2 * cfg.NPO * 256], BF16))
    tbp_t = es.enter_context(nc.sbuf_tensor("ropebp", [128, 256], F32))
    banks = [es.enter_context(nc.psum_tensor(f"bk{i}", [128, 512], F32)) for i in range(8)]

    def f32v(ap):
        return ap.bitcast(F32)

    ds_c = DSem(nc, "const")
    ds_c2 = DSem(nc, "const2")
    t_cb, t_vec, t_gb, t_one32, t_pt = Tile("cb"), Tile("vec"), Tile("gb"), Tile("one32"), Tile("pt")
    S.dma(POOL, ds_c2, [lambda e: e.dma_start(out=cb_t[:, :], in_=d_cb[:, :])], writes=[t_cb])
    S.dma(SP, ds_c, [lambda e: e.dma_start(out=vec_t[:, 0:cfg.NV], in_=d_vecs[:, :])], writes=[t_vec])
    S.dma(SP, ds_c, [lambda e: e.dma_start(out=gb_t[:, :], in_=d_gbias[:, :])], writes=[t_gb])
    S.dma(SP, ds_c, [lambda e: e.dma_start(out=one32_t[:, :], in_=d_ones32[:, :])], writes=[t_one32])
    t_tbp = Tile("tbp")
    S.dma(SP, ds_c, [lambda e: e.dma_start(out=tbp_t[:, :], in_=d_ropeBp[:, :])], writes=[t_tbp])
    S.dma(POOL, ds_c2, [lambda e: e.dma_start(
        out=pt_t[:, :].rearrange("p (k t) -> p k t", k=2),
        in_=d_pt.rearrange("(k p) t -> p k t", p=128))], writes=[t_pt])
    ev_const = (ds_c.sem, ds_c.val, "dma_const")
    for t in (t_vec, t_gb, t_one32, t_tbp):
        t.w = ev_const
    ev_const2 = (ds_c2.sem, ds_c2.val, "dma_const2")
    for t in (t_cb, t_pt):
        t.w = ev_const2

    def cbs(c0, n=128, p0=0, p1=128):
        return cb_t[p0:p1, c0:c0 + n]

    TRI, TRI2, TRI2F = cbs(cfg.c_tri), cbs(cfg.c_tri2), cbs(cfg.c_tri2f)
    IDB, ONESB = cbs(cfg.c_id), cbs(cfg.c_ones)
    NEGM = cbs(cfg.c_negm)
    ONI = [cbs(cfg.c_on0), cbs(cfg.c_on1)]

    def vcol(c):
        return vec_t[:, c:c + 1]

    v_ag1 = cfg.NV
    v_esk = cfg.NV + 2 * DC
    t_vec2 = Tile("vec2")
    S.op(DVE, lambda e: e.tensor_scalar(out=vec_t[:, v_ag1:v_ag1 + 2 * DC], in0=vec_t[:, cfg.v_l1g:cfg.v_l1g + 2 * DC],
                                        scalar1=float(cfg.alpha), scalar2=None, op0=ALU.mult),
         reads=[t_vec], writes=[t_vec2])
    S.op(ACT, lambda e: e.activation(out=vec_t[:, v_esk:v_esk + BC], in_=vec_t[:, cfg.v_sink:cfg.v_sink + BC], func=AF.Exp),
         reads=[t_vec], writes=[t_vec2])

    ring_tiles = [Tile(f"ring{i}") for i in range(cfg.ring)]
    ring_ds = [DSem(nc, f"ring{i}") for i in range(cfg.ring)]
    ring_pos = [0]

    def wload(dram_ap, n):
        i = ring_pos[0] % cfg.ring
        ring_pos[0] += 1
        dst = ring_t[:, i * SL:i * SL + n]
        S.dma(POOL, ring_ds[i], [lambda e: e.dma_start(out=dst, in_=dram_ap)], writes=[ring_tiles[i]])
        return dst, ring_tiles[i]

    pending = []

    def wtile(name, idx, nk):
        pcs = []
        for k0 in range(0, nk, KPS):
            k1 = min(nk, k0 + KPS)
            ap, tl = wload(d_w[name][idx, :, k0 * 128:k1 * 128], (k1 - k0) * 128)
            pcs.append((ap, tl, k0, k1))
        while pending:
            pending.pop(0)()
        return pcs

    def mm_group(out_ap, out_tile, pcs, rhs, n_extra_before=0, stop=True, start=True, col0=0):
        nk = pcs[-1][3]
        for (ap, tl, k0, k1) in pcs:
            for k in range(k0, k1):
                r_ap, r_tl = rhs(k)
                lhs = ap[:, (k - k0) * 128 + col0:(k - k0) * 128 + col0 + 128]
                st = start and (k == 0)
                sp_ = stop and (k == nk - 1)
                S.op(PE, lambda e, lhs=lhs, r_ap=r_ap, st=st, sp_=sp_: e.matmul(out_ap, lhsT=lhs, rhs=r_ap, start=st, stop=sp_),
                     reads=[tl, r_tl], writes=[out_tile], sig=(k == k1 - 1))

    KG = (KC + 3) // 4
    xt_tile = [[Tile(f"xt{tg}_{g}") for g in range(KG)] for tg in range(2)]
    ds_x = [[DSem(nc, f"x{tg}_{g}") for g in range(KG)] for tg in range(2)]

    def a_ap(tg, kc, c0=0, n=TG):
        o = (tg * KC + kc) * TG
        return A_t[:, o + c0:o + c0 + n]

    def load_xt(src, tgs=(0, 1)):
        for tg in tgs:
            for g in range(KG):
                k0 = g * 4
                nk = min(4, KC - k0)
                o = (tg * KC + k0) * TG
                S.dma(POOL, ds_x[tg][g], [lambda e, o=o, nk=nk, k0=k0, tg=tg: e.dma_start(
                    out=A_t[:, o:o + nk * TG].rearrange("p (k t) -> p k t", k=nk),
                    in_=src[k0 * 128:(k0 + nk) * 128, tg * TG:(tg + 1) * TG].rearrange("(k p) t -> p k t", p=128))],
                    writes=[xt_tile[tg][g]])

    def xrhs(tg, c0=0, n=TG):
        return lambda k: (a_ap(tg, k, c0, n), xt_tile[tg][k // 4])

    kap_tile = [Tile(f"kap{i}") for i in range(AH + 1)]
    vap_tile = [Tile(f"vap{i}") for i in range(AH)]

    def kap_ap(slot, c0=0, n=1024):
        return B_t[:, slot * 1024 + c0:slot * 1024 + c0 + n]

    VAP0 = (AH + 1) * 1024

    def vap_ap(h, c0=0, n=1024):
        return B_t[:, VAP0 + h * 1024 + c0:VAP0 + h * 1024 + c0 + n]

    t_tab = Tile("tab")
    ds_tab = DSem(nc, "tab")
    TAB = f32v(C_t[:, 0:4096])

    def load_tab(src):
        S.dma(SP, ds_tab, [lambda e: e.dma_start(out=TAB[:, 0:1024], in_=src[:, 0:1024]),
                           lambda e: e.dma_start(out=TAB[:, 1024:2048], in_=src[:, 1024:2048])], writes=[t_tab])

    cpos = [4096]
    coff = {}

    def ctake(n, key=None):
        o = cpos[0]
        cpos[0] += n
        assert cpos[0] <= 16384, cpos[0]
        if key is not None:
            coff[key] = o
        return C_t[:, o:o + n]

    ppos = [0]

    def ptake(n):
        o = ppos[0]
        ppos[0] += n
        assert ppos[0] <= P_N, ppos[0]
        return P_t[:, o:o + n]

    Qh = [ctake(1024) for _ in range(2)]
    Kown = [ctake(1024, "k0"), ctake(1024, "k1")]
    Vown = [ctake(1024) for _ in range(2)]
    MaskT = [ctake(1024) for _ in range(2)]
    VTt = [ctake(512) for _ in range(2)]
    RT1 = f32v(ctake(1024))
    ctake(1152)
    RT2 = f32v(ptake(1024))
    NPT = 4
    PT = [ptake(512) for _ in range(NPT)]
    RS = [f32v(ptake(1024)) for _ in range(2)]
    GBs = f32v(ptake(128))
    M8 = f32v(ptake(128))
    MK = f32v(ptake(128))
    MKB = ptake(64)
    KSUM = f32v(ptake(16))
    KMB = ptake(8)
    t_Qh = [Tile("Qh0"), Tile("Qh1")]
    t_Kown = [Tile("Ko0"), Tile("Ko1")]
    t_Vown = [Tile("Vo0"), Tile("Vo1")]
    t_MaskT = [Tile("Mt0"), Tile("Mt1")]
    t_VTt = [Tile("vt0"), Tile("vt1")]
    t_RT1, t_RT2a, t_RT2b = Tile("rt1"), Tile("rt2a"), Tile("rt2b")
    t_PT = [Tile(f"pt{i}") for i in range(NPT)]
    t_RS = [Tile("rs0"), Tile("rs1")]
    t_GBs, t_MKB, t_KSUM, t_KMB = Tile("gbs"), Tile("mkb"), Tile("ksum"), Tile("kmb")
    t_M8q = [Tile(f"m8_{i}") for i in range(8)]
    t_MKq = [Tile(f"mk_{i}") for i in range(8)]
    t_KSUMb = Tile("ksumb")
    t_M8c = Tile("m8c")

    p_proj = [(banks[0][:, :], Tile("pp0")), (banks[1][:, :], Tile("pp1"))]
    NS = 3
    p_s = [(banks[bi][:, :], Tile(f"ps{bi}")) for bi in (2, 3, 6)]
    p_pv = [((banks[4], banks[5]), Tile("pv0"))]
    t_b7 = Tile("pb7")
    p_vtr = (banks[7][:, 0:256].bitcast(BF16), t_b7)
    p_gate = (banks[7][:, 0:64], t_b7)
    p_mtr = (banks[7][:, :].bitcast(BF16), t_b7)
    proj_i = [0]

    def next_proj():
        r = p_proj[proj_i[0] % 2]
        proj_i[0] += 1
        return r

    def rope_evac(ps_ap, ps_tl, tcol, n, out_ap, out_tl, tabs=None):
        if tabs is None:
            cos = TAB[:, tcol:tcol + n]
            ss = TAB[:, 1024 + tcol:1024 + tcol + n]
            tab_tl = t_tab
        else:
            cos, ss, tab_tl = tabs
        S.op(DVE, lambda e: e.tensor_tensor(out=RT1[:, 0:n], in0=ps_ap, in1=cos, op=ALU.mult),
             reads=[ps_tl, tab_tl], writes=[t_RT1])
        S.op(DVE, lambda e: e.tensor_tensor(out=RT2[0:64, 0:n], in0=ps_ap[64:128, :], in1=ss[64:128, :], op=ALU.mult),
             reads=[ps_tl, tab_tl], writes=[t_RT2a])
        S.op(DVE, lambda e: e.tensor_tensor(out=RT2[64:128, 0:n], in0=ps_ap[0:64, :], in1=ss[0:64, :], op=ALU.mult),
             reads=[ps_tl, tab_tl], writes=[t_RT2b])
        S.op(DVE, lambda e: e.tensor_tensor(out=out_ap, in0=RT1[:, 0:n], in1=RT2[:, 0:n], op=ALU.add),
             reads=[t_RT1, t_RT2a, t_RT2b], writes=[out_tl])

    def v_copy(ps_ap, ps_tl, n, vt_i):
        S.op(ACT, lambda e: e.copy(out=VTt[vt_i][:, 0:n], in_=ps_ap), reads=[ps_tl], writes=[t_VTt[vt_i]])

    def v_tr(n, vt_i, out_ap, out_tl):
        nt = n // 128
        for j in range(nt):
            S.op(PE, lambda e, j=j: e.transpose(out=p_vtr[0][:, j * 128:(j + 1) * 128], in_=VTt[vt_i][:, j * 128:(j + 1) * 128], identity=IDB),
                 reads=[t_VTt[vt_i], t_cb], writes=[p_vtr[1]], sig=(j == nt - 1))
        S.op(ACT, lambda e: e.copy(out=out_ap, in_=p_vtr[0][:, 0:n]), reads=[p_vtr[1]], writes=[out_tl])

    def v_tr2(out_ap, out_tl):
        for tg in range(2):
            for j in range(4):
                c = (tg * 4 + j) * 128
                S.op(PE, lambda e, tg=tg, j=j, c=c: e.transpose(out=p_mtr[0][:, c:c + 128], in_=VTt[tg][:, j * 128:(j + 1) * 128], identity=IDB),
                     reads=[t_VTt[tg], t_cb], writes=[p_mtr[1]], sig=(tg == 1 and j == 3))
        S.op(ACT, lambda e: e.copy(out=out_ap, in_=p_mtr[0][:, 0:1024]), reads=[p_mtr[1]], writes=[out_tl])

    def v_evac(ps_ap, ps_tl, n, vt_i, out_ap, out_tl):
        v_copy(ps_ap, ps_tl, n, vt_i)
        v_tr(n, vt_i, out_ap, out_tl)

    load_xt(d_xp, (0,))
    pending.append(lambda: load_xt(d_xp, (1,)))
    load_tab(d_ropeAp)
    vt_i = [0]
    def p0_k(h):
        pk = wtile("wka", h, KC)
        for tg in range(2):
            ps_ap, ps_tl = next_proj()
            mm_group(ps_ap, ps_tl, pk, xrhs(tg))
            rope_evac(ps_ap, ps_tl, tg * TG, TG, kap_ap(h + 1, tg * TG, TG), kap_tile[h + 1])

    def p0_v(h):
        pv = wtile("wva", h, KC)
        for tg in range(2):
            ps_ap, ps_tl = next_proj()
            mm_group(ps_ap, ps_tl, pv, xrhs(tg))
            v_copy(ps_ap, ps_tl, TG, tg)

    def p0_vtr(h):
        v_tr2(vap_ap(h, 0, 1024), vap_tile[h])

    t_kvb = Tile("kvb")

    def p0_bprev():
        for g in range(NKV):
            pk = wtile("wkb", g, KC)
            ps_ap, ps_tl = next_proj()
            mm_group(ps_ap[:, 0:128], ps_tl, pk, xrhs(1, TG - 128, 128))
            rope_evac(ps_ap[:, 0:128], ps_tl, 0, 128, kvb_t[:, g * 128:(g + 1) * 128], t_kvb,
                      tabs=(tbp_t[:, 0:128], tbp_t[:, 128:256], t_tbp))
            pv = wtile("wvb", g, KC)
            ps_ap, ps_tl = next_proj()
            mm_group(ps_ap[:, 0:128], ps_tl, pv, xrhs(1, TG - 128, 128))
            v_evac(ps_ap[:, 0:128], ps_tl, 128, vt_i[0] % 2, kvb_t[:, (NKV + g) * 128:(NKV + g + 1) * 128], t_kvb)
            vt_i[0] += 1


    for h in range(AH):
        p0_k(h)
        if h > 0:
            p0_vtr(h - 1)
        if h == AH // 2:
            p0_bprev()
        if h == AH - 1:
            pv = wtile("wva", h, KC)
            for tg in range(2):
                ps_ap, ps_tl = next_proj()
                mm_group(ps_ap, ps_tl, pv, xrhs(tg))
                v_copy(ps_ap, ps_tl, TG, tg)
                load_xt(d_xo, (tg,))
        else:
            p0_v(h)
    p0_vtr(AH - 1)

    load_tab(d_ropeAo)
    SCA = 128.0 ** -0.5

    def projA_qk(h):
        hb = h % 2
        pq = wtile("wqa", h, KC)
        for tg in range(2):
            ps_ap, ps_tl = next_proj()
            mm_group(ps_ap, ps_tl, pq, xrhs(tg))
            rope_evac(ps_ap, ps_tl, tg * TG, TG, Qh[hb][:, tg * TG:(tg + 1) * TG], t_Qh[hb])
        pk = wtile("wka", h, KC)
        for tg in range(2):
            ps_ap, ps_tl = next_proj()
            mm_group(ps_ap, ps_tl, pk, xrhs(tg))
            rope_evac(ps_ap, ps_tl, tg * TG, TG, Kown[hb][:, tg * TG:(tg + 1) * TG], t_Kown[hb])

    def projA_v(h):
        pv = wtile("wva", h, KC)
        for tg in range(2):
            ps_ap, ps_tl = next_proj()
            mm_group(ps_ap, ps_tl, pv, xrhs(tg))
            v_copy(ps_ap, ps_tl, TG, tg)

    def vtrA(h):
        hb = h % 2
        v_tr2(Vown[hb][:, 0:1024], t_Vown[hb])

    def ksumA(h):
        hb = h % 2
        S.op(DVE, lambda e: e.tensor_reduce(out=KSUM[:, 0:4], in_=kap_ap(h + 1).rearrange("p (n k) -> p n k", k=256), axis=AX.X, op=ALU.add),
             reads=[kap_tile[h + 1]], writes=[t_KSUM])
        S.op(DVE, lambda e: e.tensor_reduce(out=KSUM[:, 4:8], in_=Kown[hb].rearrange("p (n k) -> p n k", k=256), axis=AX.X, op=ALU.add),
             reads=[t_Kown[hb]], writes=[t_KSUMb])
        S.op(DVE, lambda e: e.tensor_copy(out=KMB[:, 0:8], in_=KSUM[:, 0:8]), reads=[t_KSUM, t_KSUMb], writes=[t_KMB])

    def prepA1(h):
        hb = h % 2
        for qt in range(8):
            S.op(PE, lambda e, qt=qt: e.matmul(p_gate[0][:, qt * 8:(qt + 1) * 8], lhsT=Qh[hb][:, qt * 128:(qt + 1) * 128], rhs=KMB[:, 0:8], start=True, stop=True),
                 reads=[t_Qh[hb], t_KMB], writes=[p_gate[1]], sig=(qt == 7))
        S.op(DVE, lambda e: e.tensor_tensor(out=GBs[:, 0:64], in0=p_gate[0], in1=gb_t[:, 0:64], op=ALU.add),
             reads=[p_gate[1], t_gb], writes=[t_GBs])
        for qt in range(8):
            S.op(DVE, lambda e, qt=qt: e.max(out=M8[:, qt * 8:(qt + 1) * 8], in_=GBs[:, qt * 8:(qt + 1) * 8]),
                 reads=[t_GBs], writes=[t_M8q[qt]], extra=[t_M8c.w] + list(t_M8c.r))
        S.op(DVE, lambda e: e.tensor_scalar(out=M8[:, 0:64], in0=M8[:, 0:64], scalar1=-1e29, scalar2=None, op0=ALU.max),
             reads=t_M8q, writes=[t_M8c])
        for qt in range(8):
            S.op(DVE, lambda e, qt=qt: e.tensor_scalar(out=MK[:, qt * 8:(qt + 1) * 8], in0=GBs[:, qt * 8:(qt + 1) * 8],
                                                       scalar1=M8[:, qt * 8 + 2:qt * 8 + 3], scalar2=-NEG, op0=ALU.is_ge, op1=ALU.mult),
                 reads=[t_GBs, t_M8c], writes=[t_MKq[qt]])
        S.op(DVE, lambda e: e.tensor_tensor(out=MKB[:, 0:64], in0=MK[:, 0:64], in1=gb_t[:, 64:128], op=ALU.add),
             reads=t_MKq + [t_gb], writes=[t_MKB])

    def prepA2(h):
        hb = h % 2
        for qt in range(8):
            S.op(PE, lambda e, qt=qt: e.transpose(out=p_mtr[0][0:8, qt * 128:(qt + 1) * 128], in_=MKB[:, qt * 8:(qt + 1) * 8], identity=IDB),
                 reads=[t_MKB, t_cb], writes=[p_mtr[1]], sig=(qt == 7))
        S.op(ACT, lambda e: e.copy(out=MaskT[hb][0:8, :], in_=p_mtr[0][0:8, :]), reads=[p_mtr[1]], writes=[t_MaskT[hb]])

    pt_i = [0]
    s_i = [0]
    pv_i = [0]

    PT2 = [C_t[:, 14336 + i * 512:14336 + (i + 1) * 512] for i in range(2)]
    t_PT2 = [Tile("pt2_0"), Tile("pt2_1")]

    def attn_tiles(tiles, pv_ap, pv_tl, nq, scale, fin, pair_sum=False):
        n = len(tiles)
        pend = []
        sumq = []

        def emit_s(tt):
            sl_ap, sl_tl = p_s[s_i[0] % NS]
            s_i[0] += 1
            c0, nn = tt["qc0"], tt["nqt"]
            o = sl_ap[:, c0:c0 + nn]
            nm = len(tt["masks"])
            S.op(PE, lambda e: e.matmul(o, lhsT=tt["k_ap"], rhs=tt["q_ap"], start=True, stop=(nm == 0)),
                 reads=[tt["k_tl"], tt["q_tl"]], writes=[sl_tl], sig=(nm == 0))
            for mi, (ml, mr, mtl, mc0, mn) in enumerate(tt["masks"]):
                mo = sl_ap[:, mc0:mc0 + mn]
                if len(mr.shape) == 3:
                    mo = mo.rearrange("p (c q) -> p c q", q=128)
                S.op(PE, lambda e, ml=ml, mr=mr, mo=mo, mi=mi: e.matmul(mo, lhsT=ml, rhs=mr, start=False, stop=(mi == nm - 1)),
                     reads=mtl, writes=[sl_tl], sig=(mi == nm - 1))
            pi = pt_i[0] % NPT
            pt_i[0] += 1
            S.op(ACT, lambda e: e.activation(out=PT[pi][:, c0:c0 + nn], in_=o, func=AF.Exp, scale=float(scale)),
                 reads=[sl_tl], writes=[t_PT[pi]])
            return pi

        def emit_sum(k2, on_ap, first, last):
            S.op(PE, lambda e: e.matmul(pv_ap[1][:, 0:nq], lhsT=on_ap, rhs=PT2[k2][:, 0:nq], start=first, stop=last),
                 reads=[t_cb, t_PT2[k2]], writes=[pv_tl], sig=True)

        def emit_pv(idx, tt, pi):
            c0, nn = tt["qc0"], tt["nqt"]
            first, last = (idx == 0), (idx == n - 1)
            S.op(PE, lambda e: e.matmul(pv_ap[0][:, c0:c0 + nn], lhsT=tt["v_ap"], rhs=PT[pi][:, c0:c0 + nn], start=first, stop=last),
                 reads=[tt["v_tl"], t_PT[pi]], writes=[pv_tl], sig=pair_sum)
            if not pair_sum:
                S.op(PE, lambda e: e.matmul(pv_ap[1][:, c0:c0 + nn], lhsT=tt["on_ap"], rhs=PT[pi][:, c0:c0 + nn], start=first, stop=last),
                     reads=[t_cb, t_PT[pi]], writes=[pv_tl], sig=True)
            elif idx % 2 == 1:
                k2 = (idx // 2) % 2
                pa = pend[idx - 1]
                S.op(DVE, lambda e: e.tensor_tensor(out=PT2[k2][:, 0:nq], in0=PT[pa][:, 0:nq], in1=PT[pi][:, 0:nq], op=ALU.add),
                     reads=[t_PT[pa], t_PT[pi]], writes=[t_PT2[k2]])
                if sumq:
                    emit_sum(*sumq.pop(0))
                sumq.append((k2, tt["on_ap"], idx == 1, idx == n - 1))

        LOOK = 2
        assert not pair_sum or (n % 2 == 0 and all(t["nqt"] == nq and t["qc0"] == 0 for t in tiles))
        for i in range(n + LOOK):
            if i < n:
                pend.append(emit_s(tiles[i]))
            if i >= LOOK:
                emit_pv(i - LOOK, tiles[i - LOOK], pend[i - LOOK])
        while sumq:
            emit_sum(*sumq.pop(0))
        fin()

    def attnA(h):
        hb = h % 2
        for pr in range(2):
            j0 = 2 * pr
            qc = pr * 512
            q_ap = Qh[hb][:, qc:qc + 512]
            tiles = []
            for n_ in range(4 + j0 + 2):
                jj = n_ - 4 - j0
                for half_ in range(2):
                    kt = 2 * n_ + half_
                    if kt < 8:
                        k_ap, k_tl = kap_ap(h + 1, kt * 128, 128), kap_tile[h + 1]
                        v_ap, v_tl = vap_ap(h, kt * 128, 128), vap_tile[h]
                    else:
                        k_ap, k_tl = Kown[hb][:, (kt - 8) * 128:(kt - 7) * 128], t_Kown[hb]
                        v_ap, v_tl = Vown[hb][:, (kt - 8) * 128:(kt - 7) * 128], t_Vown[hb]
                    masks = [(cb_t[:, cfg.c_E + n_ * 128:cfg.c_E + (n_ + 1) * 128], MaskT[hb][:, qc:qc + 512],
                              [t_cb, t_MaskT[hb]], 0, 512)]
                    if jj in (0, 1):
                        base = jj * 256
                        if half_ == 0:
                            masks.append((IDB, TRI, [t_cb], base, 128))
                        else:
                            masks.append((IDB, NEGM, [t_cb], base, 128))
                            masks.append((IDB, TRI, [t_cb], base + 128, 128))
                    tiles.append(dict(k_ap=k_ap, k_tl=k_tl, q_ap=q_ap, q_tl=t_Qh[hb], qc0=0, nqt=512,
                                      masks=masks, v_ap=v_ap, v_tl=v_tl, on_ap=ONESB))
            bank, pv_tl = p_pv[0]
            ri = pv_i[0] % 2
            pv_i[0] += 1

            def fin(bank=bank, pv_tl=pv_tl, ri=ri, qc=qc):
                S.op(DVE, lambda e: e.tensor_copy(out=RS[0][:, 0:512], in_=bank[1][:, 0:512]), reads=[pv_tl], writes=[t_RS[0]])
                S.op(DVE, lambda e: e.tensor_copy(out=RS[1][:, 0:512], in_=bank[0][:, 0:512]), reads=[pv_tl], writes=[t_RS[1]])
                S.op(DVE, lambda e: e.reciprocal(out=RS[0][:, 0:512], in_=RS[0][:, 0:512]), reads=[t_RS[0]], writes=[t_RS[0]])
                S.op(DVE, lambda e: e.tensor_tensor(out=kap_ap(h, qc, 512), in0=RS[1][:, 0:512], in1=RS[0][:, 0:512], op=ALU.mult),
                     reads=[t_RS[0], t_RS[1]], writes=[kap_tile[h]])
            attn_tiles(tiles, bank, pv_tl, 512, SCA, fin, pair_sum=True)

    for i_ in range(2):
        S.op(DVE, lambda e, i_=i_: e.memset(MaskT[i_][:, :], 0.0), writes=[t_MaskT[i_]])
    projA_qk(0)
    ksumA(0)
    projA_v(0)
    vtrA(0)
    for h in range(AH):
        prepA1(h)
        if h + 1 < AH:
            projA_qk(h + 1)
            ksumA(h + 1)
        prepA2(h)
        if h + 1 < AH:
            projA_v(h + 1)
        attnA(h)
        if h + 1 < AH:
            vtrA(h + 1)

    if cfg.debug:
        ds_dbg = DSem(nc, "dbg")
        dbg_st = es.enter_context(nc.sbuf_tensor("dbgst", [128, 1024], F32))
        t_dbg = Tile("dbgst")

        def dump(dst, src_ap, src_tl, n):
            S.op(DVE, lambda e: e.tensor_copy(out=dbg_st[:, 0:n], in_=src_ap), reads=[src_tl], writes=[t_dbg])
            S.dma(SP, ds_dbg, [lambda e: e.dma_start(out=dst, in_=dbg_st[:, 0:n])], reads=[t_dbg])
        for h in range(AH):
            dump(dbg["dbg_ya"][h * 128:(h + 1) * 128, :], kap_ap(h), kap_tile[h], 1024)

    bar = S.barrier()
    t_tab.r = list(bar)
    load_tab(d_ropeBo)
    SCB = 64.0 ** -0.5
    CPG = cfg.CPG
    Q4 = C_t[:, 4096:4096 + CPG * 1024]
    KBv = [C_t[:, 8192 + i * 1152:8192 + (i + 1) * 1152] for i in range(2)]
    VBv = [C_t[:, 10496:11648], C_t[:, 14336:15488]]
    assert 4096 + CPG * 1024 <= 8192
    t_Q4 = Tile("q4", bar)
    t_KBv = [Tile("kbv0", bar), Tile("kbv1", bar)]
    t_VBv = [Tile("vbv0", bar), Tile("vbv1", bar)]
    t_ybt = vap_tile
    for tl in t_PT + t_RS + t_VTt + [t_RT1, t_RT2a, t_RT2b]:
        tl.r = list(bar)
    v3 = lambda ap: ap.rearrange("p (t c) -> p t c", c=128)
    for i in range(2):
        S.op(DVE, lambda e, i=i: e.memset(VBv[i], 0.0), writes=[t_VBv[i]])

    def rope_evac4(ps_ap, ps_tl, tcol, ci, tg):
        n = TG
        cos = TAB[:, tcol:tcol + n]
        ss = TAB[:, 1024 + tcol:1024 + tcol + n]
        S.op(DVE, lambda e: e.tensor_tensor(out=RT1[:, 0:n], in0=ps_ap, in1=cos, op=ALU.mult),
             reads=[ps_tl, t_tab], writes=[t_RT1])
        S.op(DVE, lambda e: e.tensor_tensor(out=RT2[0:64, 0:n], in0=ps_ap[64:128, :], in1=ss[64:128, :], op=ALU.mult),
             reads=[ps_tl, t_tab], writes=[t_RT2a])
        S.op(DVE, lambda e: e.tensor_tensor(out=RT2[64:128, 0:n], in0=ps_ap[0:64, :], in1=ss[0:64, :], op=ALU.mult),
             reads=[ps_tl, t_tab], writes=[t_RT2b])
        q4v = Q4.rearrange("p (t c q) -> p t c q", c=CPG, q=128)[:, tg * 4:(tg + 1) * 4, ci, :]
        r3 = lambda ap: ap.rearrange("p (t q) -> p t q", q=128)
        S.op(DVE, lambda e: e.tensor_tensor(out=q4v, in0=r3(RT1[:, 0:n]), in1=r3(RT2[:, 0:n]), op=ALU.add),
             reads=[t_RT1, t_RT2a, t_RT2b], writes=[t_Q4])

    def projQ4(g):
        for ci in range(CPG):
            pq = wtile("wqb", g * CPG + ci, KC)
            for tg in range(2):
                ps_ap, ps_tl = next_proj()
                mm_group(ps_ap, ps_tl, pq, xrhs(tg))
                rope_evac4(ps_ap, ps_tl, tg * TG, ci, tg)

    def prepKVB(g):
        pk = wtile("wkb", g, KC)
        for tg in range(2):
            ps_ap, ps_tl = next_proj()
            mm_group(ps_ap, ps_tl, pk, xrhs(tg))
            rope_evac(ps_ap, ps_tl, tg * TG, TG, KBv[0][:, 128 + tg * TG:128 + (tg + 1) * TG], t_KBv[0])
        S.op(DVE, lambda e: e.tensor_scalar(out=KBv[1][:, 128:1152], in0=KBv[0][:, 128:1152], scalar1=vcol(cfg.v_slot + 1), scalar2=None, op0=ALU.mult),
             reads=[t_KBv[0], t_vec], writes=[t_KBv[1]])
        S.op(DVE, lambda e: e.tensor_scalar(out=KBv[0][:, 128:1152], in0=KBv[0][:, 128:1152], scalar1=vcol(cfg.v_slot + 0), scalar2=None, op0=ALU.mult),
             reads=[t_KBv[0], t_vec], writes=[t_KBv[0]])
        for i in range(2):
            S.op(DVE, lambda e, i=i: e.tensor_scalar(out=KBv[i][:, 0:128], in0=kvb_t[:, g * 128:(g + 1) * 128], scalar1=vcol(cfg.v_slot + i), scalar2=None, op0=ALU.mult),
                 reads=[t_kvb, t_vec], writes=[t_KBv[i]])
        pv = wtile("wvb", g, KC)
        for tg in range(2):
            ps_ap, ps_tl = next_proj()
            mm_group(ps_ap, ps_tl, pv, xrhs(tg))
            v_copy(ps_ap, ps_tl, TG, tg)
        for tg in range(2):
            for j in range(4):
                S.op(PE, lambda e, j=j, tg=tg: e.transpose(out=p_vtr[0][:, j * 128:(j + 1) * 128], in_=VTt[tg][:, j * 128:(j + 1) * 128], identity=IDB),
                     reads=[t_VTt[tg], t_cb], writes=[p_vtr[1]], sig=(j == 3))
            t0 = 1 + tg * 4
            S.op(ACT, lambda e, t0=t0: e.copy(out=v3(VBv[0])[:, t0:t0 + 4, 0:64], in_=v3(p_vtr[0])[:, :, 0:64]), reads=[p_vtr[1]], writes=[t_VBv[0]])
            S.op(ACT, lambda e, t0=t0: e.copy(out=v3(VBv[1])[:, t0:t0 + 4, 64:128], in_=v3(p_vtr[0])[:, :, 64:128]), reads=[p_vtr[1]], writes=[t_VBv[1]])
        vprev = kvb_t[:, (NKV + g) * 128:(NKV + g + 1) * 128]
        S.op(ACT, lambda e: e.copy(out=VBv[0][:, 0:64], in_=vprev[:, 0:64]), reads=[t_kvb], writes=[t_VBv[0]])
        S.op(ACT, lambda e: e.copy(out=VBv[1][:, 64:128], in_=vprev[:, 64:128]), reads=[t_kvb], writes=[t_VBv[1]])

    def bcast4(ap128):
        return ap128.unsqueeze(1).broadcast_to([128, CPG, 128])

    def attnB4(g):
        c0 = g * CPG
        for qt in range(8):
            q_ap = Q4[:, qt * CPG * 128:(qt + 1) * CPG * 128]
            tiles = []
            for i in range(2):
                for which in range(2):
                    ktl = qt + which
                    msk = TRI if which == 1 else (TRI2F if qt == 0 else TRI2)
                    masks = [(IDB, bcast4(msk), [t_cb], 0, CPG * 128)]
                    tiles.append(dict(k_ap=KBv[i][:, ktl * 128:(ktl + 1) * 128], k_tl=t_KBv[i], q_ap=q_ap, q_tl=t_Q4, qc0=0, nqt=CPG * 128,
                                      masks=masks,
                                      v_ap=VBv[i][:, ktl * 128:(ktl + 1) * 128], v_tl=t_VBv[i], on_ap=ONI[i]))
            bank, pv_tl = p_pv[0]
            NQ = CPG * 128

            def fin(bank=bank, pv_tl=pv_tl, qt=qt):
                S.op(DVE, lambda e: e.tensor_copy(out=RS[0][:, 0:NQ], in_=bank[1][:, 0:NQ]), reads=[pv_tl], writes=[t_RS[0]])
                S.op(DVE, lambda e: e.tensor_copy(out=RS[1][:, 0:NQ], in_=bank[0][:, 0:NQ]), reads=[pv_tl], writes=[t_RS[1]])
                for ci in range(CPG):
                    S.op(DVE, lambda e, ci=ci: e.tensor_scalar(out=RS[0][:, ci * 128:(ci + 1) * 128], in0=RS[0][:, ci * 128:(ci + 1) * 128],
                                                               scalar1=vec_t[:, v_esk + c0 + ci:v_esk + c0 + ci + 1], scalar2=None, op0=ALU.add),
                         reads=[t_RS[0], t_vec2], writes=[t_RS[0]])
                S.op(DVE, lambda e: e.reciprocal(out=RS[0][:, 0:NQ], in_=RS[0][:, 0:NQ]), reads=[t_RS[0]], writes=[t_RS[0]])
                yv = B_t[:, VAP0 + c0 * 1024:VAP0 + (c0 + CPG) * 1024].rearrange("p (c t) -> p c t", t=1024)[:, :, qt * 128:(qt + 1) * 128]
                r3 = lambda ap: ap.rearrange("p (c q) -> p c q", q=128)
                S.op(DVE, lambda e: e.tensor_tensor(out=yv, in0=r3(RS[1][:, 0:NQ]), in1=r3(RS[0][:, 0:NQ]), op=ALU.mult),
                     reads=[t_RS[0], t_RS[1]], writes=[t_ybt[c0 + ci] for ci in range(CPG)])
            attn_tiles(tiles, bank, pv_tl, NQ, SCB, fin)

    for g in range(NKV):
        projQ4(g)
        prepKVB(g)
        attnB4(g)

    if cfg.debug:
        for c in range(BC):
            dump(dbg["dbg_yb"][c * 128:(c + 1) * 128, :], vap_ap(c), vap_tile[c], 1024)

    bar = S.barrier()
    pb = [Tile(f"pb{i}", bar) for i in range(8)]
    c_tile = [Tile(f"c{i}", bar) for i in range(32)]
    p2 = [Tile(f"p2_{i}", bar) for i in range(6)]

    def c_ap(i, n=512):
        return C_t[:, i * 512:i * 512 + n]

    def p2_ap(i):
        return f32v(P_t[:, i * 1024:(i + 1) * 1024])

    S1A, S2A = p2_ap(2), p2_ap(3)
    t_s1a, t_s2a = p2[2], p2[3]

    def ya_rhs(tg):
        return lambda k: (kap_ap(k, tg * TG, TG), kap_tile[k])

    def yb_rhs(tg):
        return lambda k: (vap_ap(k, tg * TG, TG), vap_tile[k])

    def mg_ap(tg, c):
        return (c_ap(c), c_tile[c]) if tg == 0 else (a_ap(0, c), xt_tile[0][c // 4])

    for tg in range(2):
        for c in range(DC):
            b0 = 4 * (c % 2)
            pga = wtile("wga", c, KC)
            mm_group(banks[b0][:, :], pb[b0], pga, xrhs(tg))
            pgb = wtile("wgb", c, KC)
            mm_group(banks[b0 + 1][:, :], pb[b0 + 1], pgb, xrhs(tg))
            pua = wtile("wupa", c, AH)
            mm_group(banks[b0 + 2][:, :], pb[b0 + 2], pua, ya_rhs(tg))
            pub = wtile("wupb", c, BC)
            mm_group(banks[b0 + 3][:, :], pb[b0 + 3], pub, yb_rhs(tg))
            S.op(ACT, lambda e, b0=b0, c=c: e.activation(out=p2_ap(0), in_=banks[b0][:, :], func=AF.Sigmoid, bias=vcol(cfg.v_bga + c), scale=1.0),
                 reads=[pb[b0], t_vec], writes=[p2[0]])
            S.op(ACT, lambda e, b0=b0, c=c: e.activation(out=p2_ap(1), in_=banks[b0 + 1][:, :], func=AF.Sigmoid, bias=vcol(cfg.v_bgb + c), scale=1.0),
                 reads=[pb[b0 + 1], t_vec], writes=[p2[1]])
            S.op(DVE, lambda e, b0=b0: e.tensor_tensor(out=p2_ap(2), in0=banks[b0 + 2][:, :], in1=p2_ap(0), op=ALU.mult),
                 reads=[pb[b0 + 2], p2[0]], writes=[p2[2]])
            S.op(DVE, lambda e, b0=b0: e.tensor_tensor(out=p2_ap(3), in0=banks[b0 + 3][:, :], in1=p2_ap(1), op=ALU.mult),
                 reads=[pb[b0 + 3], p2[1]], writes=[p2[3]])
            m_ap, m_tl = mg_ap(tg, c)
            S.op(DVE, lambda e, m_ap=m_ap: e.tensor_tensor(out=m_ap, in0=p2_ap(2), in1=p2_ap(3), op=ALU.add),
                 reads=[p2[2], p2[3]], writes=[m_tl])
            if cfg.debug:
                dump(dbg["dbg_mg"][c * 128:(c + 1) * 128, tg * TG:(tg + 1) * TG], m_ap, m_tl, TG)

    bar = S.barrier()
    bq = [Tile(f"bq{i}", bar) for i in range(DC)]

    def bq_ap(oc):
        return f32v(B_t[:, oc * 1024:(oc + 1) * 1024])

    ds_x32 = [DSem(nc, "x32_0"), DSem(nc, "x32_1")]
    ds_ple = [DSem(nc, "ple0"), DSem(nc, "ple1")]
    t_ple = [Tile("ple0"), Tile("ple1")]
    ple_i = [0]
    ds_out = [DSem(nc, "out0"), DSem(nc, "out1")]

    def ln_finish(stat_s1, stat_s2, gcol, bcol, bcol2, dbg_name, tg, scale_out):
        MEAN, RSTD, TMP = p2_ap(2), p2_ap(3), p2_ap(4)
        S.op(DVE, lambda e: e.tensor_scalar(out=MEAN, in0=banks[stat_s1][:, :], scalar1=1.0 / D, scalar2=None, op0=ALU.mult),
             reads=[pb[stat_s1]], writes=[p2[2]])
        S.op(DVE, lambda e: e.tensor_tensor(out=TMP, in0=MEAN, in1=MEAN, op=ALU.mult), reads=[p2[2]], writes=[p2[4]])
        S.op(DVE, lambda e: e.scalar_tensor_tensor(out=RSTD, in0=banks[stat_s2][:, :], scalar=1.0 / D, in1=TMP, op0=ALU.mult, op1=ALU.subtract),
             reads=[pb[stat_s2], p2[4]], writes=[p2[3]])
        S.op(ACT, lambda e: e.activation(out=RSTD, in_=RSTD, func=AF.Sqrt, bias=vcol(cfg.v_eps), scale=1.0),
             reads=[p2[3], t_vec], writes=[p2[3]])
        S.op(DVE, lambda e: e.reciprocal(out=RSTD, in_=RSTD), reads=[p2[3]], writes=[p2[3]])
        S.op(DVE, lambda e: e.scalar_tensor_tensor(out=MEAN, in0=MEAN, scalar=-1.0, in1=RSTD, op0=ALU.mult, op1=ALU.mult),
             reads=[p2[2], p2[3]], writes=[p2[2]])
        for oc in range(DC):
            ti = 4 + (oc % 2)
            gc = vec_t[:, gcol + oc:gcol + oc + 1]
            S.op(DVE, lambda e, oc=oc, ti=ti, gc=gc: e.scalar_tensor_tensor(out=p2_ap(ti), in0=bq_ap(oc), scalar=gc, in1=RSTD, op0=ALU.mult, op1=ALU.mult),
                 reads=[bq[oc], p2[3], t_vec, t_vec2], writes=[p2[ti]])
            S.op(DVE, lambda e, oc=oc, ti=ti, gc=gc: e.scalar_tensor_tensor(out=p2_ap(ti), in0=MEAN, scalar=gc, in1=p2_ap(ti), op0=ALU.mult, op1=ALU.add),
                 reads=[p2[2], p2[ti], t_vec, t_vec2], writes=[p2[ti]])
            S.op(ACT, lambda e, oc=oc, ti=ti: e.activation(out=a_ap(1, oc), in_=p2_ap(ti), func=AF.Identity, bias=vec_t[:, bcol2 + oc:bcol2 + oc + 1], scale=float(scale_out)),
                 reads=[p2[ti], t_vec], writes=[xt_tile[1][oc // 4]])
            S.op(ACT, lambda e, oc=oc, ti=ti: e.activation(out=bq_ap(oc), in_=p2_ap(ti), func=AF.Identity, bias=vec_t[:, bcol + oc:bcol + oc + 1], scale=1.0),
                 reads=[p2[ti], t_vec, t_vec2], writes=[bq[oc]])
            if cfg.debug:
                S.op(ACT, lambda e, oc=oc: e.activation(out=dbg_st[:, 0:TG], in_=bq_ap(oc), func=AF.Identity, scale=float(scale_out)),
                     reads=[bq[oc]], writes=[t_dbg])
                S.dma(SP, ds_dbg, [lambda e, oc=oc: e.dma_start(out=dbg[dbg_name][oc * 128:(oc + 1) * 128, tg * TG:(tg + 1) * TG], in_=dbg_st[:, 0:TG])], reads=[t_dbg])

    def stats_acc(oc, sq_i):
        S.op(ACT, lambda e: e.activation(out=p2_ap(sq_i), in_=bq_ap(oc), func=AF.Square), reads=[bq[oc]], writes=[p2[sq_i]])
        if oc == 0:
            S.op(DVE, lambda e: e.tensor_copy(out=S1A, in_=bq_ap(oc)), reads=[bq[oc]], writes=[t_s1a])
            S.op(DVE, lambda e: e.tensor_copy(out=S2A, in_=p2_ap(sq_i)), reads=[p2[sq_i]], writes=[t_s2a])
        else:
            S.op(DVE, lambda e: e.tensor_tensor(out=S1A, in0=S1A, in1=bq_ap(oc), op=ALU.add), reads=[bq[oc], t_s1a], writes=[t_s1a])
            S.op(DVE, lambda e: e.tensor_tensor(out=S2A, in0=S2A, in1=p2_ap(sq_i), op=ALU.add), reads=[p2[sq_i], t_s2a], writes=[t_s2a])

    def stats_mm(s1, s2):
        S.op(PE, lambda e: e.matmul(banks[s1][:, :], lhsT=one32_t[:, :], rhs=S1A, start=True, stop=True),
             reads=[t_one32, t_s1a], writes=[pb[s1]], sig=True)
        S.op(PE, lambda e: e.matmul(banks[s2][:, :], lhsT=one32_t[:, :], rhs=S2A, start=True, stop=True),
             reads=[t_one32, t_s2a], writes=[pb[s2]], sig=True)

    for tg in range(2):
        def mrhs(k, tg=tg):
            return mg_ap(tg, k)
        for oc in range(DC):
            pw = wtile("wo", oc, DC)
            b = oc % 2
            mm_group(banks[b][:, :], pb[b], pw, mrhs)
            xi = oc % 2
            S.dma(SP, ds_x32[xi], [lambda e, oc=oc, xi=xi: e.dma_start(out=p2_ap(xi), in_=d_xo[oc * 128:(oc + 1) * 128, tg * TG:(tg + 1) * TG])],
                  writes=[p2[xi]])
            S.op(DVE, lambda e, oc=oc, xi=xi, b=b: e.scalar_tensor_tensor(out=bq_ap(oc), in0=p2_ap(xi), scalar=float(cfg.alpha), in1=banks[b][:, :],
                                                                     op0=ALU.mult, op1=ALU.add),
                 reads=[p2[xi], pb[b]], writes=[bq[oc]])
            stats_acc(oc, 4 + (oc % 2))
        stats_mm(2, 3)
        ln_finish(2, 3, v_ag1, v_ag1 + DC, cfg.v_l1b, "dbg_h1", tg, 1.0 / cfg.alpha)

        def hrhs(k):
            return (a_ap(1, k), xt_tile[1][k // 4])
        ub_i = [0]

        def ffn_up(fg):
            for j in range(cfg.FGS):
                fc = fg * cfg.FGS + j
                pw = wtile("wffu", fc, KC)
                b = ub_i[0] % 2
                ub_i[0] += 1
                mm_group(banks[b][:, :], pb[b], pw, hrhs)
                ui = (fg % 2) * cfg.FGS + j
                S.op(ACT, lambda e, b=b: e.activation(out=p2_ap(b), in_=banks[b][:, :], func=AF.Relu), reads=[pb[b]], writes=[p2[b]])
                S.op(DVE, lambda e, b=b, ui=ui: e.tensor_tensor(out=c_ap(ui), in0=p2_ap(b), in1=p2_ap(b), op=ALU.mult),
                     reads=[p2[b]], writes=[c_tile[ui]])

        db_i = [0]

        def ffn_down(fg):
            for ocp in range(DC // 2):
                ap, tl = wload(d_w["wffd"][fg * (DC // 2) + ocp, :, :], 2 * cfg.FGS * 128)
                for oi in range(2):
                    oc = ocp * 2 + oi
                    b = 4 + db_i[0] % 4
                    db_i[0] += 1
                    for j in range(cfg.FGS):
                        ui = (fg % 2) * cfg.FGS + j
                        lhs = ap[:, (oi * cfg.FGS + j) * 128:(oi * cfg.FGS + j + 1) * 128]
                        S.op(PE, lambda e, lhs=lhs, ui=ui, b=b, j=j: e.matmul(banks[b][:, :], lhsT=lhs, rhs=c_ap(ui), start=(j == 0), stop=(j == cfg.FGS - 1)),
                             reads=[tl, c_tile[ui]], writes=[pb[b]], sig=(j == cfg.FGS - 1))
                    S.op(DVE, lambda e, oc=oc, b=b: e.tensor_tensor(out=bq_ap(oc), in0=banks[b][:, :], in1=bq_ap(oc), op=ALU.add),
                         reads=[pb[b], bq[oc]], writes=[bq[oc]])
                    if fg == cfg.FG - 1:
                        stats_acc(oc, 4 + (oc % 2))

        ffn_up(0)
        for fg in range(cfg.FG):
            if fg + 1 < cfg.FG:
                ffn_up(fg + 1)
            ffn_down(fg)
        stats_mm(2, 3)
        ln_finish(2, 3, cfg.v_l2g, cfg.v_l2b, cfg.v_l2b, "dbg_h2", tg, 1.0)

        for oc in range(DC):
            if oc % cfg.NPO == 0:
                pi_ = ple_i[0] % 2
                ple_i[0] += 1
                pl_ap, pl_tl = ple_t[:, pi_ * cfg.NPO * 256:(pi_ + 1) * cfg.NPO * 256], t_ple[pi_]
                S.dma(POOL, ds_ple[pi_], [lambda e, oc=oc, pl_ap=pl_ap: e.dma_start(out=pl_ap, in_=d_w["wple"][oc // cfg.NPO, :, :])], writes=[pl_tl])
            pw = wtile("wpg", oc, KC)
            b = oc % 2
            mm_group(banks[b][:, :], pb[b], pw, hrhs)
            b2 = 4 + oc % 2
            o_ = oc % cfg.NPO
            for kc2 in range(2):
                lhs = pl_ap[:, (o_ * 2 + kc2) * 128:(o_ * 2 + kc2 + 1) * 128]
                S.op(PE, lambda e, lhs=lhs, kc2=kc2, b2=b2: e.matmul(banks[b2][:, :], lhsT=lhs, rhs=pt_t[:, kc2 * T + tg * TG:kc2 * T + (tg + 1) * TG], start=(kc2 == 0), stop=(kc2 == 1)),
                     reads=[pl_tl, t_pt], writes=[pb[b2]], sig=(kc2 == 1))
            si = oc % 2
            S.op(ACT, lambda e, b=b, si=si: e.activation(out=p2_ap(si), in_=banks[b][:, :], func=AF.Sigmoid), reads=[pb[b]], writes=[p2[si]])
            S.op(DVE, lambda e, b2=b2, si=si: e.tensor_tensor(out=p2_ap(si), in0=banks[b2][:, :], in1=p2_ap(si), op=ALU.mult),
                 reads=[pb[b2], p2[si]], writes=[p2[si]])
            oi_ = 2 + oc % 2
            S.op(DVE, lambda e, oc=oc, si=si, oi_=oi_: e.tensor_tensor(out=p2_ap(oi_), in0=p2_ap(si), in1=bq_ap(oc), op=ALU.add),
                 reads=[p2[si], bq[oc]], writes=[p2[oi_]])
            S.dma(SP, ds_out[oc % 2], [lambda e, oc=oc, oi_=oi_: e.dma_start(out=d_out[oc * 128:(oc + 1) * 128, tg * TG:(tg + 1) * TG], in_=p2_ap(oi_))],
                  reads=[p2[oi_]])

    for ds in ds_out:
        SP.e.wait_ge(ds.sem, ds.val)
    if cfg.debug:
        SP.e.wait_ge(ds_dbg.sem, ds_dbg.val)
    es.close()
    return nc


_CACHE = {}


def kernel(**inputs):
    cfg = Cfg(D=4096)
    sh = prep_shared(cfg, inputs)
    in_maps = [prep_core(cfg, inputs, sh, c) for c in range(8)]
    if "nc" not in _CACHE:
        _CACHE["nc"] = build(cfg)
    res = run_bass_kernel_spmd(_CACHE["nc"], in_maps, core_ids=list(range(8)))
    B, SEQ, T = 4, cfg.SEQ, cfg.T
    out = np.empty((B, SEQ, cfg.D), np.float32)
    for c in range(8):
        b, half = c // 2, c % 2
        out[b, half * T:(half + 1) * T, :] = res.results[c]["out_t"].T
    return out
```

```python
import numpy as np
import concourse.bass as bass
import concourse.mybir as mybir
from concourse.bass_utils import run_bass_kernel_spmd

F32 = mybir.dt.float32
BF16 = mybir.dt.bfloat16
AF = mybir.ActivationFunctionType
ALU = mybir.AluOpType
AX = mybir.AxisListType

NEG = -30000.0
ROPE_THETA = 10000.0
LN_EPS = 1e-5


class Cfg:
    def __init__(self, D=4096, debug=False, ring=4):
        self.D = D
        self.KC = D // 128
        self.DC = D // 128
        self.AW = D // 2
        self.AH = self.AW // 128
        self.BW = D // 2
        self.BQH = self.BW // 64
        self.BC = self.BW // 128
        self.NKV = self.BQH // 8
        self.CPG = self.BC // self.NKV
        self.KVW = self.NKV * 64
        self.FF = 4 * D
        self.FC = self.FF // 128
        self.FGS = 8
        self.FG = self.FC // self.FGS
        self.PLE = 256
        self.SEQ = 2048
        self.T = 1024
        self.TG = 512
        self.NTG = 2
        self.KPS = 16
        self.SL = self.KPS * 128
        self.ring = ring
        self.NPO = min(2, self.DC)
        self.alpha = 2.0 ** 0.25
        self.debug = debug
        c = 0
        self.v_bga = c; c += self.DC
        self.v_bgb = c; c += self.DC
        self.v_l1g = c; c += self.DC
        self.v_l1b = c; c += self.DC
        self.v_l2g = c; c += self.DC
        self.v_l2b = c; c += self.DC
        self.v_sink = c; c += self.BC
        self.v_slot = c; c += 2
        self.v_eps = c; c += 1
        self.NV = c
        self.NG = 128
        c = 0
        self.c_tri = c; c += 128
        self.c_tri2 = c; c += 128
        self.c_tri2f = c; c += 128
        self.c_id = c; c += 128
        self.c_ones = c; c += 128
        self.c_on0 = c; c += 128
        self.c_on1 = c; c += 128
        self.c_E = c; c += 8 * 128
        self.c_negm = c; c += 128
        self.NCB = c


class Tile:
    __slots__ = ("name", "w", "r")

    def __init__(self, name, init=()):
        self.name = name
        self.w = None
        self.r = list(init)


class Eng:
    def __init__(self, nc, eng, name, is_pe=False):
        self.e = eng
        self.name = name
        self.sem = nc.alloc_semaphore(name="s_" + name)
        self.n = 0
        self.waited = {}
        self.is_pe = is_pe

    def wait(self, ev):
        sem, val, _ = ev
        k = id(sem)
        if self.waited.get(k, 0) >= val:
            return
        self.e.wait_ge(sem, val)
        self.waited[k] = val

    def last(self):
        return (self.sem, self.n, self.name)


class DSem:
    def __init__(self, nc, name):
        self.sem = nc.alloc_semaphore(name="d_" + name)
        self.val = 0
        self.name = name


class Sched:
    def __init__(self, nc):
        self.nc = nc
        self.pe = Eng(nc, nc.tensor, "pe", True)
        self.act = Eng(nc, nc.scalar, "act")
        self.dve = Eng(nc, nc.vector, "dve")
        self.pool = Eng(nc, nc.gpsimd, "pool")
        self.sp = Eng(nc, nc.sync, "sp")
        self.nops = 0

    def _deps(self, E, reads, writes, extra):
        for ev in extra:
            if ev is not None and ev[1] > 0:
                E.wait(ev)
        for t in reads:
            if t.w is not None and not (E.is_pe and t.w[2] == "pe"):
                E.wait(t.w)
        for t in writes:
            if t.w is not None and not (E.is_pe and t.w[2] == "pe"):
                E.wait(t.w)
            for ev in t.r:
                if not (E.is_pe and ev[2] == "pe"):
                    E.wait(ev)

    @staticmethod
    def _upd(ev, reads, writes):
        for t in writes:
            t.w = ev
            t.r = []
        for t in reads:
            t.r = [x for x in t.r if x[0] is not ev[0]] + [ev]

    def op(self, E, fn, reads=(), writes=(), sig=True, extra=()):
        assert sig or E.is_pe
        self._deps(E, reads, writes, extra)
        ins = fn(E.e)
        self.nops += 1
        if sig:
            E.n += 1
            ins.then_inc(E.sem, 1)
            ev = (E.sem, E.n, E.name)
        else:
            ev = (E.sem, E.n + 1, E.name)
        self._upd(ev, reads, writes)
        return ev

    def dma(self, Q, ds, fns, reads=(), writes=(), extra=()):
        self._deps(Q, reads, writes, extra)
        for fn in fns:
            fn(Q.e).then_inc(ds.sem, 16)
            ds.val += 16
            self.nops += 1
        ev = (ds.sem, ds.val, "dma_" + ds.name)
        self._upd(ev, reads, writes)
        return ev

    def barrier(self):
        evs = [self.pe.last(), self.act.last(), self.dve.last()]
        for E in (self.pe, self.act, self.dve):
            for ev in evs:
                if ev[1] > 0 and ev[2] != E.name:
                    E.wait(ev)
        return [ev for ev in evs if ev[1] > 0]


def _tiles_from_cols(W, col_lists):
    K = W.shape[0]
    kc = K // 128
    out = np.empty((len(col_lists), 128, kc * 128), np.float32)
    for i, cols in enumerate(col_lists):
        blk = W[:, cols]
        out[i] = blk.reshape(kc, 128, 128).transpose(1, 0, 2).reshape(128, kc * 128)
    return out


def _tiles_contig(W, c0, n):
    K = W.shape[0]
    kc = K // 128
    blk = W[:, c0:c0 + n * 128].reshape(kc, 128, n, 128)
    return np.ascontiguousarray(blk.transpose(2, 1, 0, 3)).reshape(n, 128, kc * 128)


def prep_shared(cfg, inp):
    D, AW, BW, KVW = cfg.D, cfg.AW, cfg.BW, cfg.KVW
    w_in = np.asarray(inp["w_in"][0], np.float32)
    o_qa, o_ka, o_va = 0, AW, 2 * AW
    o_qb = 3 * AW
    o_kb = o_qb + BW
    o_vb = o_kb + KVW
    o_ga = o_vb + KVW
    o_gb = o_ga + D
    sh = {}
    sh["wqa"] = _tiles_contig(w_in, o_qa, cfg.AH)
    sh["wka"] = _tiles_contig(w_in, o_ka, cfg.AH)
    sh["wva"] = _tiles_contig(w_in, o_va, cfg.AH)
    qb_cols = []
    for c in range(cfg.BC):
        h0, h1 = o_qb + (2 * c) * 64, o_qb + (2 * c + 1) * 64
        qb_cols.append(np.concatenate([np.arange(h0, h0 + 32), np.arange(h1, h1 + 32),
                                       np.arange(h0 + 32, h0 + 64), np.arange(h1 + 32, h1 + 64)]))
    sh["wqb"] = _tiles_from_cols(w_in, qb_cols)
    kb_cols, vb_cols = [], []
    for g in range(cfg.NKV):
        k0 = o_kb + g * 64
        kb_cols.append(np.concatenate([np.arange(k0, k0 + 32), np.arange(k0, k0 + 32),
                                       np.arange(k0 + 32, k0 + 64), np.arange(k0 + 32, k0 + 64)]))
        v0 = o_vb + g * 64
        vb_cols.append(np.concatenate([np.arange(v0, v0 + 64), np.arange(v0, v0 + 64)]))
    sh["wkb"] = _tiles_from_cols(w_in, kb_cols)
    sh["wvb"] = _tiles_from_cols(w_in, vb_cols)
    sh["wga"] = _tiles_contig(w_in, o_ga, cfg.DC)
    sh["wgb"] = _tiles_contig(w_in, o_gb, cfg.DC)
    sh["wupa"] = _tiles_contig(np.asarray(inp["w_up_a"][0], np.float32), 0, cfg.DC)
    sh["wupb"] = _tiles_contig(np.asarray(inp["w_up_b"][0], np.float32), 0, cfg.DC)
    sh["wo"] = _tiles_contig(np.asarray(inp["w_o"][0], np.float32), 0, cfg.DC)
    sh["wffu"] = _tiles_contig(np.asarray(inp["w_ff_up"][0], np.float32), 0, cfg.FC)
    wd = np.asarray(inp["w_ff_down"][0], np.float32).reshape(cfg.FG, cfg.FGS, 128, cfg.DC // 2, 2, 128)
    sh["wffd"] = np.ascontiguousarray(wd.transpose(0, 3, 2, 4, 1, 5)).reshape(
        cfg.FG * (cfg.DC // 2), 128, 2 * cfg.FGS * 128)
    sh["wpg"] = _tiles_contig(np.asarray(inp["w_ple_gate"][0], np.float32), 0, cfg.DC)
    wp = np.asarray(inp["w_ple"][0], np.float32).reshape(2, 128, cfg.DC // cfg.NPO, cfg.NPO, 128)
    sh["wple"] = np.ascontiguousarray(wp.transpose(2, 1, 3, 0, 4)).reshape(
        cfg.DC // cfg.NPO, 128, cfg.NPO * 2 * 128)
    vec = np.zeros((128, cfg.NV), np.float32)
    pm = lambda v: np.asarray(v, np.float32).reshape(cfg.DC, 128).T
    vec[:, cfg.v_bga:cfg.v_bga + cfg.DC] = pm(inp["b_gate"][0, 0])
    vec[:, cfg.v_bgb:cfg.v_bgb + cfg.DC] = pm(inp["b_gate"][0, 1])
    vec[:, cfg.v_l1g:cfg.v_l1g + cfg.DC] = pm(inp["ln1_g"][0])
    vec[:, cfg.v_l1b:cfg.v_l1b + cfg.DC] = pm(inp["ln1_b"][0])
    vec[:, cfg.v_l2g:cfg.v_l2g + cfg.DC] = pm(inp["ln2_g"][0])
    vec[:, cfg.v_l2b:cfg.v_l2b + cfg.DC] = pm(inp["ln2_b"][0])
    sinks = np.asarray(inp["sinks"][0], np.float32)
    for c in range(cfg.BC):
        vec[0:64, cfg.v_sink + c] = sinks[2 * c]
        vec[64:128, cfg.v_sink + c] = sinks[2 * c + 1]
    p = np.arange(128)
    vec[:, cfg.v_slot + 0] = ((p % 64) < 32).astype(np.float32)
    vec[:, cfg.v_slot + 1] = ((p % 64) >= 32).astype(np.float32)
    vec[:, cfg.v_eps] = LN_EPS
    sh["vecs"] = vec
    cb = np.zeros((128, cfg.NCB), np.float32)
    k = np.arange(128)[:, None]
    q = np.arange(128)[None, :]
    cb[:, cfg.c_tri:cfg.c_tri + 128] = np.where(k <= q, 0.0, NEG)
    cb[:, cfg.c_tri2:cfg.c_tri2 + 128] = np.where(k > q, 0.0, NEG)
    cb[:, cfg.c_id:cfg.c_id + 128] = np.eye(128)
    cb[:, cfg.c_ones:cfg.c_ones + 128] = 1.0
    cb[:, cfg.c_on0:cfg.c_on0 + 64] = 1.0
    cb[:, cfg.c_on1 + 64:cfg.c_on1 + 128] = 1.0
    for n in range(8):
        cb[n, cfg.c_E + n * 128:cfg.c_E + (n + 1) * 128] = 1.0
    cb[:, cfg.c_negm:cfg.c_negm + 128] = NEG
    sh["cb"] = cb
    sh["ones32"] = np.ones((128, 128), np.float32)
    return sh


def prep_core(cfg, inp, sh, core):
    b, half = core // 2, core % 2
    T = cfg.T
    x = inp["x"]
    m = dict(sh)
    m["xt_own"] = np.ascontiguousarray(np.asarray(x[b, half * T:(half + 1) * T, :], np.float32).T)
    if half == 1:
        m["xt_prev"] = np.ascontiguousarray(np.asarray(x[b, 0:T, :], np.float32).T)
    else:
        m["xt_prev"] = np.zeros((cfg.D, T), np.float32)
    m["pt"] = np.ascontiguousarray(np.asarray(inp["p"][0, b, half * T:(half + 1) * T, :], np.float32).T)
    pos = (half * T - T + np.arange(2 * T)).astype(np.float64)
    pidx = np.arange(128)
    fa = ROPE_THETA ** (-(pidx % 64).astype(np.float64) / 64.0)
    fb = ROPE_THETA ** (-(pidx % 32).astype(np.float64) / 32.0)
    sgn = np.where(pidx < 64, 1.0, -1.0)[:, None]
    def tabs(f):
        ang = (pos.astype(np.float32)[None, :] * f.astype(np.float32)[:, None]).astype(np.float32)
        return np.cos(ang).astype(np.float32), (np.sin(ang) * sgn).astype(np.float32)
    ca, sa = tabs(fa)
    cbt, sbt = tabs(fb)
    m["ropeAp"] = np.ascontiguousarray(np.concatenate([ca[:, :T], sa[:, :T]], axis=1))
    m["ropeAo"] = np.ascontiguousarray(np.concatenate([ca[:, T:], sa[:, T:]], axis=1))
    m["ropeBo"] = np.ascontiguousarray(np.concatenate([cbt[:, T:], sbt[:, T:]], axis=1))
    m["ropeBp"] = np.ascontiguousarray(np.concatenate([cbt[:, T - 128:T], sbt[:, T - 128:T]], axis=1))
    gb = np.zeros((128, cfg.NG), np.float32)
    for qt in range(8):
        j = qt // 2
        for n in range(8):
            valid = (n < 4 + j) and (half == 1 or n >= 4)
            gb[:, qt * 8 + n] = 0.0 if valid else -1e30
            gb[:, 64 + qt * 8 + n] = 0.0 if n == 4 + j else NEG
    m["gbias"] = gb
    cbm = sh["cb"].copy()
    k = np.arange(128)[:, None]
    q = np.arange(128)[None, :]
    cbm[:, cfg.c_tri2f:cfg.c_tri2f + 128] = np.where(k > q, 0.0, NEG) if half == 1 else NEG
    m["cb"] = cbm
    return m


def build(cfg):
    nc = bass.Bass("TRN2", target_bir_lowering=False)
    D, KC, DC, AH, BC, NKV, T, TG = cfg.D, cfg.KC, cfg.DC, cfg.AH, cfg.BC, cfg.NKV, cfg.T, cfg.TG
    KPS, SL = cfg.KPS, cfg.SL

    def din(name, shape):
        return nc.dram_tensor(name, list(shape), F32, kind="ExternalInput").ap()

    d_xo = din("xt_own", [D, T])
    d_xp = din("xt_prev", [D, T])
    d_pt = din("pt", [cfg.PLE, T])
    d_ropeAp = din("ropeAp", [128, 2 * T])
    d_ropeAo = din("ropeAo", [128, 2 * T])
    d_ropeBo = din("ropeBo", [128, 2 * T])
    d_ropeBp = din("ropeBp", [128, 256])
    d_gbias = din("gbias", [128, cfg.NG])
    d_cb = din("cb", [128, cfg.NCB])
    d_vecs = din("vecs", [128, cfg.NV])
    d_ones32 = din("ones32", [128, 128])
    d_w = {}
    for nm, nt, L in (("wqa", AH, KC * 128), ("wka", AH, KC * 128), ("wva", AH, KC * 128),
                      ("wqb", BC, KC * 128), ("wkb", NKV, KC * 128), ("wvb", NKV, KC * 128),
                      ("wga", DC, KC * 128), ("wgb", DC, KC * 128),
                      ("wupa", DC, AH * 128), ("wupb", DC, BC * 128), ("wo", DC, DC * 128),
                      ("wffu", cfg.FC, KC * 128), ("wffd", cfg.FG * (DC // 2), 2 * cfg.FGS * 128),
                      ("wpg", DC, KC * 128), ("wple", DC // cfg.NPO, cfg.NPO * 2 * 128)):
        d_w[nm] = din(nm, [nt, 128, L])
    d_out = nc.dram_tensor("out_t", [D, T], F32, kind="ExternalOutput").ap()
    dbg = {}
    if cfg.debug:
        for nm, shp in (("dbg_ya", [cfg.AW, T]), ("dbg_yb", [cfg.BW, T]), ("dbg_mg", [D, T]),
                        ("dbg_h1", [D, T]), ("dbg_h2", [D, T])):
            dbg[nm] = nc.dram_tensor(nm, shp, F32, kind="ExternalOutput").ap()

    S = Sched(nc)
    PE, ACT, DVE, POOL, SP = S.pe, S.act, S.dve, S.pool, S.sp

    A_N = cfg.NTG * KC * TG
    B_N = max((2 * AH + 1) * 1024, DC * 1024)
    C_N = max(DC * 512, 16384)
    P_N = 6144
    from contextlib import ExitStack
    es = ExitStack()
    ring_t = es.enter_context(nc.sbuf_tensor("ring", [128, cfg.ring * SL], BF16))
    A_t = es.enter_context(nc.sbuf_tensor("arA", [128, A_N], BF16))
    B_t = es.enter_context(nc.sbuf_tensor("arB", [128, B_N], BF16))
    C_t = es.enter_context(nc.sbuf_tensor("arC", [128, C_N], BF16))
    P_t = es.enter_context(nc.sbuf_tensor("arP", [128, P_N], BF16))
    cb_t = es.enter_context(nc.sbuf_tensor("cbb", [128, cfg.NCB], BF16))
    vec_t = es.enter_context(nc.sbuf_tensor("vecs_sb", [128, cfg.NV + 2 * DC + BC], F32))
    gb_t = es.enter_context(nc.sbuf_tensor("gbias_sb", [128, cfg.NG], F32))
    one32_t = es.enter_context(nc.sbuf_tensor("ones32_sb", [128, 128], F32))
    pt_t = es.enter_context(nc.sbuf_tensor("ptb", [128, 2 * T], BF16))
    kvb_t = es.enter_context(nc.sbuf_tensor("kvbprev", [128, 2 * NKV * 128], BF16))
    ple_t = es.enter_context(nc.sbuf_tensor("pleslot", [128, 2 * cfg.NPO * 256], BF16))
    tbp_t = es.enter_context(nc.sbuf_tensor("ropebp", [128, 256], F32))
    banks = [es.enter_context(nc.psum_tensor(f"bk{i}", [128, 512], F32)) for i in range(8)]

    def f32v(ap):
        return ap.bitcast(F32)

    ds_c = DSem(nc, "const")
    ds_c2 = DSem(nc, "const2")
    t_cb, t_vec, t_gb, t_one32, t_pt = Tile("cb"), Tile("vec"), Tile("gb"), Tile("one32"), Tile("pt")
    S.dma(POOL, ds_c2, [lambda e: e.dma_start(out=cb_t[:, :], in_=d_cb[:, :])], writes=[t_cb])
    S.dma(SP, ds_c, [lambda e: e.dma_start(out=vec_t[:, 0:cfg.NV], in_=d_vecs[:, :])], writes=[t_vec])
    S.dma(SP, ds_c, [lambda e: e.dma_start(out=gb_t[:, :], in_=d_gbias[:, :])], writes=[t_gb])
    S.dma(SP, ds_c, [lambda e: e.dma_start(out=one32_t[:, :], in_=d_ones32[:, :])], writes=[t_one32])
    t_tbp = Tile("tbp")
    S.dma(SP, ds_c, [lambda e: e.dma_start(out=tbp_t[:, :], in_=d_ropeBp[:, :])], writes=[t_tbp])
    S.dma(POOL, ds_c2, [lambda e: e.dma_start(
        out=pt_t[:, :].rearrange("p (k t) -> p k t", k=2),
        in_=d_pt.rearrange("(k p) t -> p k t", p=128))], writes=[t_pt])
    ev_const = (ds_c.sem, ds_c.val, "dma_const")
    for t in (t_vec, t_gb, t_one32, t_tbp):
        t.w = ev_const
    ev_const2 = (ds_c2.sem, ds_c2.val, "dma_const2")
    for t in (t_cb, t_pt):
        t.w = ev_const2

    def cbs(c0, n=128, p0=0, p1=128):
        return cb_t[p0:p1, c0:c0 + n]

    TRI, TRI2, TRI2F = cbs(cfg.c_tri), cbs(cfg.c_tri2), cbs(cfg.c_tri2f)
    IDB, ONESB = cbs(cfg.c_id), cbs(cfg.c_ones)
    NEGM = cbs(cfg.c_negm)
    ONI = [cbs(cfg.c_on0), cbs(cfg.c_on1)]

    def vcol(c):
        return vec_t[:, c:c + 1]

    v_ag1 = cfg.NV
    v_esk = cfg.NV + 2 * DC
    t_vec2 = Tile("vec2")
    S.op(DVE, lambda e: e.tensor_scalar(out=vec_t[:, v_ag1:v_ag1 + 2 * DC], in0=vec_t[:, cfg.v_l1g:cfg.v_l1g + 2 * DC],
                                        scalar1=float(cfg.alpha), scalar2=None, op0=ALU.mult),
         reads=[t_vec], writes=[t_vec2])
    S.op(ACT, lambda e: e.activation(out=vec_t[:, v_esk:v_esk + BC], in_=vec_t[:, cfg.v_sink:cfg.v_sink + BC], func=AF.Exp),
         reads=[t_vec], writes=[t_vec2])

    ring_tiles = [Tile(f"ring{i}") for i in range(cfg.ring)]
    ring_ds = [DSem(nc, f"ring{i}") for i in range(cfg.ring)]
    ring_pos = [0]

    def wload(dram_ap, n):
        i = ring_pos[0] % cfg.ring
        ring_pos[0] += 1
        dst = ring_t[:, i * SL:i * SL + n]
        S.dma(POOL, ring_ds[i], [lambda e: e.dma_start(out=dst, in_=dram_ap)], writes=[ring_tiles[i]])
        return dst, ring_tiles[i]

    pending = []

    def wtile(name, idx, nk):
        pcs = []
        for k0 in range(0, nk, KPS):
            k1 = min(nk, k0 + KPS)
            ap, tl = wload(d_w[name][idx, :, k0 * 128:k1 * 128], (k1 - k0) * 128)
            pcs.append((ap, tl, k0, k1))
        while pending:
            pending.pop(0)()
        return pcs

    def mm_group(out_ap, out_tile, pcs, rhs, n_extra_before=0, stop=True, start=True, col0=0):
        nk = pcs[-1][3]
        for (ap, tl, k0, k1) in pcs:
            for k in range(k0, k1):
                r_ap, r_tl = rhs(k)
                lhs = ap[:, (k - k0) * 128 + col0:(k - k0) * 128 + col0 + 128]
                st = start and (k == 0)
                sp_ = stop and (k == nk - 1)
                S.op(PE, lambda e, lhs=lhs, r_ap=r_ap, st=st, sp_=sp_: e.matmul(out_ap, lhsT=lhs, rhs=r_ap, start=st, stop=sp_),
                     reads=[tl, r_tl], writes=[out_tile], sig=(k == k1 - 1))

    KG = (KC + 3) // 4
    xt_tile = [[Tile(f"xt{tg}_{g}") for g in range(KG)] for tg in range(2)]
    ds_x = [[DSem(nc, f"x{tg}_{g}") for g in range(KG)] for tg in range(2)]

    def a_ap(tg, kc, c0=0, n=TG):
        o = (tg * KC + kc) * TG
        return A_t[:, o + c0:o + c0 + n]

    def load_xt(src, tgs=(0, 1)):
        for tg in tgs:
            for g in range(KG):
                k0 = g * 4
                nk = min(4, KC - k0)
                o = (tg * KC + k0) * TG
                S.dma(POOL, ds_x[tg][g], [lambda e, o=o, nk=nk, k0=k0, tg=tg: e.dma_start(
                    out=A_t[:, o:o + nk * TG].rearrange("p (k t) -> p k t", k=nk),
                    in_=src[k0 * 128:(k0 + nk) * 128, tg * TG:(tg + 1) * TG].rearrange("(k p) t -> p k t", p=128))],
                    writes=[xt_tile[tg][g]])

    def xrhs(tg, c0=0, n=TG):
        return lambda k: (a_ap(tg, k, c0, n), xt_tile[tg][k // 4])

    kap_tile = [Tile(f"kap{i}") for i in range(AH + 1)]
    vap_tile = [Tile(f"vap{i}") for i in range(AH)]

    def kap_ap(slot, c0=0, n=1024):
        return B_t[:, slot * 1024 + c0:slot * 1024 + c0 + n]

    VAP0 = (AH + 1) * 1024

    def vap_ap(h, c0=0, n=1024):
        return B_t[:, VAP0 + h * 1024 + c0:VAP0 + h * 1024 + c0 + n]

    t_tab = Tile("tab")
    ds_tab = DSem(nc, "tab")
    TAB = f32v(C_t[:, 0:4096])

    def load_tab(src):
        S.dma(SP, ds_tab, [lambda e: e.dma_start(out=TAB[:, 0:1024], in_=src[:, 0:1024]),
                           lambda e: e.dma_start(out=TAB[:, 1024:2048], in_=src[:, 1024:2048])], writes=[t_tab])

    cpos = [4096]
    coff = {}

    def ctake(n, key=None):
        o = cpos[0]
        cpos[0] += n
        assert cpos[0] <= 16384, cpos[0]
        if key is not None:
            coff[key] = o
        return C_t[:, o:o + n]

    ppos = [0]

    def ptake(n):
        o = ppos[0]
        ppos[0] += n
        assert ppos[0] <= P_N, ppos[0]
        return P_t[:, o:o + n]

    Qh = [ctake(1024) for _ in range(2)]
    Kown = [ctake(1024, "k0"), ctake(1024, "k1")]
    Vown = [ctake(1024) for _ in range(2)]
    MaskT = [ctake(1024) for _ in range(2)]
    VTt = [ctake(512) for _ in range(2)]
    RT1 = f32v(ctake(1024))
    ctake(1152)
    RT2 = f32v(ptake(1024))
    NPT = 4
    PT = [ptake(512) for _ in range(NPT)]
    RS = [f32v(ptake(1024)) for _ in range(2)]
    GBs = f32v(ptake(128))
    M8 = f32v(ptake(128))
    MK = f32v(ptake(128))
    MKB = ptake(64)
    KSUM = f32v(ptake(16))
    KMB = ptake(8)
    t_Qh = [Tile("Qh0"), Tile("Qh1")]
    t_Kown = [Tile("Ko0"), Tile("Ko1")]
    t_Vown = [Tile("Vo0"), Tile("Vo1")]
    t_MaskT = [Tile("Mt0"), Tile("Mt1")]
    t_VTt = [Tile("vt0"), Tile("vt1")]
    t_RT1, t_RT2a, t_RT2b = Tile("rt1"), Tile("rt2a"), Tile("rt2b")
    t_PT = [Tile(f"pt{i}") for i in range(NPT)]
    t_RS = [Tile("rs0"), Tile("rs1")]
    t_GBs, t_MKB, t_KSUM, t_KMB = Tile("gbs"), Tile("mkb"), Tile("ksum"), Tile("kmb")
    t_M8q = [Tile(f"m8_{i}") for i in range(8)]
    t_MKq = [Tile(f"mk_{i}") for i in range(8)]
    t_KSUMb = Tile("ksumb")
    t_M8c = Tile("m8c")

    p_proj = [(banks[0][:, :], Tile("pp0")), (banks[1][:, :], Tile("pp1"))]
    NS = 3
    p_s = [(banks[bi][:, :], Tile(f"ps{bi}")) for bi in (2, 3, 6)]
    p_pv = [((banks[4], banks[5]), Tile("pv0"))]
    t_b7 = Tile("pb7")
    p_vtr = (banks[7][:, 0:256].bitcast(BF16), t_b7)
    p_gate = (banks[7][:, 0:64], t_b7)
    p_mtr = (banks[7][:, :].bitcast(BF16), t_b7)
    proj_i = [0]

    def next_proj():
        r = p_proj[proj_i[0] % 2]
        proj_i[0] += 1
        return r

    def rope_evac(ps_ap, ps_tl, tcol, n, out_ap, out_tl, tabs=None):
        if tabs is None:
            cos = TAB[:, tcol:tcol + n]
            ss = TAB[:, 1024 + tcol:1024 + tcol + n]
            tab_tl = t_tab
        else:
            cos, ss, tab_tl = tabs
        S.op(DVE, lambda e: e.tensor_tensor(out=RT1[:, 0:n], in0=ps_ap, in1=cos, op=ALU.mult),
             reads=[ps_tl, tab_tl], writes=[t_RT1])
        S.op(DVE, lambda e: e.tensor_tensor(out=RT2[0:64, 0:n], in0=ps_ap[64:128, :], in1=ss[64:128, :], op=ALU.mult),
             reads=[ps_tl, tab_tl], writes=[t_RT2a])
        S.op(DVE, lambda e: e.tensor_tensor(out=RT2[64:128, 0:n], in0=ps_ap[0:64, :], in1=ss[0:64, :], op=ALU.mult),
             reads=[ps_tl, tab_tl], writes=[t_RT2b])
        S.op(DVE, lambda e: e.tensor_tensor(out=out_ap, in0=RT1[:, 0:n], in1=RT2[:, 0:n], op=ALU.add),
             reads=[t_RT1, t_RT2a, t_RT2b], writes=[out_tl])

    def v_copy(ps_ap, ps_tl, n, vt_i):
        S.op(ACT, lambda e: e.copy(out=VTt[vt_i][:, 0:n], in_=ps_ap), reads=[ps_tl], writes=[t_VTt[vt_i]])

    def v_tr(n, vt_i, out_ap, out_tl):
        nt = n // 128
        for j in range(nt):
            S.op(PE, lambda e, j=j: e.transpose(out=p_vtr[0][:, j * 128:(j + 1) * 128], in_=VTt[vt_i][:, j * 128:(j + 1) * 128], identity=IDB),
                 reads=[t_VTt[vt_i], t_cb], writes=[p_vtr[1]], sig=(j == nt - 1))
        S.op(ACT, lambda e: e.copy(out=out_ap, in_=p_vtr[0][:, 0:n]), reads=[p_vtr[1]], writes=[out_tl])

    def v_tr2(out_ap, out_tl):
        for tg in range(2):
            for j in range(4):
                c = (tg * 4 + j) * 128
                S.op(PE, lambda e, tg=tg, j=j, c=c: e.transpose(out=p_mtr[0][:, c:c + 128], in_=VTt[tg][:, j * 128:(j + 1) * 128], identity=IDB),
                     reads=[t_VTt[tg], t_cb], writes=[p_mtr[1]], sig=(tg == 1 and j == 3))
        S.op(ACT, lambda e: e.copy(out=out_ap, in_=p_mtr[0][:, 0:1024]), reads=[p_mtr[1]], writes=[out_tl])

    def v_evac(ps_ap, ps_tl, n, vt_i, out_ap, out_tl):
        v_copy(ps_ap, ps_tl, n, vt_i)
        v_tr(n, vt_i, out_ap, out_tl)

    load_xt(d_xp, (0,))
    pending.append(lambda: load_xt(d_xp, (1,)))
    load_tab(d_ropeAp)
    vt_i = [0]
    def p0_k(h):
        pk = wtile("wka", h, KC)
        for tg in range(2):
            ps_ap, ps_tl = next_proj()
            mm_group(ps_ap, ps_tl, pk, xrhs(tg))
            rope_evac(ps_ap, ps_tl, tg * TG, TG, kap_ap(h + 1, tg * TG, TG), kap_tile[h + 1])

    def p0_v(h):
        pv = wtile("wva", h, KC)
        for tg in range(2):
            ps_ap, ps_tl = next_proj()
            mm_group(ps_ap, ps_tl, pv, xrhs(tg))
            v_copy(ps_ap, ps_tl, TG, tg)

    def p0_vtr(h):
        v_tr2(vap_ap(h, 0, 1024), vap_tile[h])

    t_kvb = Tile("kvb")

    def p0_bprev():
        for g in range(NKV):
            pk = wtile("wkb", g, KC)
            ps_ap, ps_tl = next_proj()
            mm_group(ps_ap[:, 0:128], ps_tl, pk, xrhs(1, TG - 128, 128))
            rope_evac(ps_ap[:, 0:128], ps_tl, 0, 128, kvb_t[:, g * 128:(g + 1) * 128], t_kvb,
                      tabs=(tbp_t[:, 0:128], tbp_t[:, 128:256], t_tbp))
            pv = wtile("wvb", g, KC)
            ps_ap, ps_tl = next_proj()
            mm_group(ps_ap[:, 0:128], ps_tl, pv, xrhs(1, TG - 128, 128))
            v_evac(ps_ap[:, 0:128], ps_tl, 128, vt_i[0] % 2, kvb_t[:, (NKV + g) * 128:(NKV + g + 1) * 128], t_kvb)
            vt_i[0] += 1


    for h in range(AH):
        if h == AH - 1:
            p0_vtr(h - 1)
            pk = wtile("wka", h, KC)
            pv = wtile("wva", h, KC)
            for tg in range(2):
                ps_ap, ps_tl = next_proj()
                mm_group(ps_ap, ps_tl, pk, xrhs(tg))
                rope_evac(ps_ap, ps_tl, tg * TG, TG, kap_ap(h + 1, tg * TG, TG), kap_tile[h + 1])
                ps_ap, ps_tl = next_proj()
                mm_group(ps_ap, ps_tl, pv, xrhs(tg))
                v_copy(ps_ap, ps_tl, TG, tg)
                load_xt(d_xo, (tg,))
            break
        p0_k(h)
        if h > 0:
            p0_vtr(h - 1)
        if h == AH // 2:
            p0_bprev()
        p0_v(h)
    p0_vtr(AH - 1)

    load_tab(d_ropeAo)
    SCA = 128.0 ** -0.5

    def projA_qk(h):
        hb = h % 2
        pq = wtile("wqa", h, KC)
        for tg in range(2):
            ps_ap, ps_tl = next_proj()
            mm_group(ps_ap, ps_tl, pq, xrhs(tg))
            rope_evac(ps_ap, ps_tl, tg * TG, TG, Qh[hb][:, tg * TG:(tg + 1) * TG], t_Qh[hb])
        pk = wtile("wka", h, KC)
        for tg in range(2):
            ps_ap, ps_tl = next_proj()
            mm_group(ps_ap, ps_tl, pk, xrhs(tg))
            rope_evac(ps_ap, ps_tl, tg * TG, TG, Kown[hb][:, tg * TG:(tg + 1) * TG], t_Kown[hb])

    def projA_v(h):
        pv = wtile("wva", h, KC)
        for tg in range(2):
            ps_ap, ps_tl = next_proj()
            mm_group(ps_ap, ps_tl, pv, xrhs(tg))
            v_copy(ps_ap, ps_tl, TG, tg)

    def vtrA(h):
        hb = h % 2
        v_tr2(Vown[hb][:, 0:1024], t_Vown[hb])

    def ksumA(h):
        hb = h % 2
        S.op(DVE, lambda e: e.tensor_reduce(out=KSUM[:, 0:4], in_=kap_ap(h + 1).rearrange("p (n k) -> p n k", k=256), axis=AX.X, op=ALU.add),
             reads=[kap_tile[h + 1]], writes=[t_KSUM])
        S.op(DVE, lambda e: e.tensor_reduce(out=KSUM[:, 4:8], in_=Kown[hb].rearrange("p (n k) -> p n k", k=256), axis=AX.X, op=ALU.add),
             reads=[t_Kown[hb]], writes=[t_KSUMb])
        S.op(DVE, lambda e: e.tensor_copy(out=KMB[:, 0:8], in_=KSUM[:, 0:8]), reads=[t_KSUM, t_KSUMb], writes=[t_KMB])

    def prepA1(h):
        hb = h % 2
        for qt in range(8):
            S.op(PE, lambda e, qt=qt: e.matmul(p_gate[0][:, qt * 8:(qt + 1) * 8], lhsT=Qh[hb][:, qt * 128:(qt + 1) * 128], rhs=KMB[:, 0:8], start=True, stop=True),
                 reads=[t_Qh[hb], t_KMB], writes=[p_gate[1]], sig=(qt == 7))
        S.op(DVE, lambda e: e.tensor_tensor(out=GBs[:, 0:64], in0=p_gate[0], in1=gb_t[:, 0:64], op=ALU.add),
             reads=[p_gate[1], t_gb], writes=[t_GBs])
        for qt in range(8):
            S.op(DVE, lambda e, qt=qt: e.max(out=M8[:, qt * 8:(qt + 1) * 8], in_=GBs[:, qt * 8:(qt + 1) * 8]),
                 reads=[t_GBs], writes=[t_M8q[qt]], extra=[t_M8c.w] + list(t_M8c.r))
        S.op(DVE, lambda e: e.tensor_scalar(out=M8[:, 0:64], in0=M8[:, 0:64], scalar1=-1e29, scalar2=None, op0=ALU.max),
             reads=t_M8q, writes=[t_M8c])
        for qt in range(8):
            S.op(DVE, lambda e, qt=qt: e.tensor_scalar(out=MK[:, qt * 8:(qt + 1) * 8], in0=GBs[:, qt * 8:(qt + 1) * 8],
                                                       scalar1=M8[:, qt * 8 + 2:qt * 8 + 3], scalar2=-NEG, op0=ALU.is_ge, op1=ALU.mult),
                 reads=[t_GBs, t_M8c], writes=[t_MKq[qt]])
        S.op(DVE, lambda e: e.tensor_tensor(out=MKB[:, 0:64], in0=MK[:, 0:64], in1=gb_t[:, 64:128], op=ALU.add),
             reads=t_MKq + [t_gb], writes=[t_MKB])

    def prepA2(h):
        hb = h % 2
        for qt in range(8):
            S.op(PE, lambda e, qt=qt: e.transpose(out=p_mtr[0][0:8, qt * 128:(qt + 1) * 128], in_=MKB[:, qt * 8:(qt + 1) * 8], identity=IDB),
                 reads=[t_MKB, t_cb], writes=[p_mtr[1]], sig=(qt == 7))
        S.op(ACT, lambda e: e.copy(out=MaskT[hb][0:8, :], in_=p_mtr[0][0:8, :]), reads=[p_mtr[1]], writes=[t_MaskT[hb]])

    pt_i = [0]
    s_i = [0]
    pv_i = [0]

    PT2 = [C_t[:, 14336 + i * 512:14336 + (i + 1) * 512] for i in range(2)]
    t_PT2 = [Tile("pt2_0"), Tile("pt2_1")]

    def attn_tiles(tiles, pv_ap, pv_tl, nq, scale, fin, pair_sum=False):
        n = len(tiles)
        pend = []
        sumq = []

        def emit_s(tt):
            sl_ap, sl_tl = p_s[s_i[0] % NS]
            s_i[0] += 1
            c0, nn = tt["qc0"], tt["nqt"]
            o = sl_ap[:, c0:c0 + nn]
            nm = len(tt["masks"])
            S.op(PE, lambda e: e.matmul(o, lhsT=tt["k_ap"], rhs=tt["q_ap"], start=True, stop=(nm == 0)),
                 reads=[tt["k_tl"], tt["q_tl"]], writes=[sl_tl], sig=(nm == 0))
            for mi, (ml, mr, mtl, mc0, mn) in enumerate(tt["masks"]):
                mo = sl_ap[:, mc0:mc0 + mn]
                if len(mr.shape) == 3:
                    mo = mo.rearrange("p (c q) -> p c q", q=128)
                S.op(PE, lambda e, ml=ml, mr=mr, mo=mo, mi=mi: e.matmul(mo, lhsT=ml, rhs=mr, start=False, stop=(mi == nm - 1)),
                     reads=mtl, writes=[sl_tl], sig=(mi == nm - 1))
            pi = pt_i[0] % NPT
            pt_i[0] += 1
            S.op(ACT, lambda e: e.activation(out=PT[pi][:, c0:c0 + nn], in_=o, func=AF.Exp, scale=float(scale)),
                 reads=[sl_tl], writes=[t_PT[pi]])
            return pi

        def emit_sum(k2, on_ap, first, last):
            S.op(PE, lambda e: e.matmul(pv_ap[1][:, 0:nq], lhsT=on_ap, rhs=PT2[k2][:, 0:nq], start=first, stop=last),
                 reads=[t_cb, t_PT2[k2]], writes=[pv_tl], sig=True)

        def emit_pv(idx, tt, pi):
            c0, nn = tt["qc0"], tt["nqt"]
            first, last = (idx == 0), (idx == n - 1)
            S.op(PE, lambda e: e.matmul(pv_ap[0][:, c0:c0 + nn], lhsT=tt["v_ap"], rhs=PT[pi][:, c0:c0 + nn], start=first, stop=last),
                 reads=[tt["v_tl"], t_PT[pi]], writes=[pv_tl], sig=pair_sum)
            if not pair_sum:
                S.op(PE, lambda e: e.matmul(pv_ap[1][:, c0:c0 + nn], lhsT=tt["on_ap"], rhs=PT[pi][:, c0:c0 + nn], start=first, stop=last),
                     reads=[t_cb, t_PT[pi]], writes=[pv_tl], sig=True)
            elif idx % 2 == 1:
                k2 = (idx // 2) % 2
                pa = pend[idx - 1]
                S.op(DVE, lambda e: e.tensor_tensor(out=PT2[k2][:, 0:nq], in0=PT[pa][:, 0:nq], in1=PT[pi][:, 0:nq], op=ALU.add),
                     reads=[t_PT[pa], t_PT[pi]], writes=[t_PT2[k2]])
                if sumq:
                    emit_sum(*sumq.pop(0))
                sumq.append((k2, tt["on_ap"], idx == 1, idx == n - 1))

        LOOK = 2
        assert not pair_sum or (n % 2 == 0 and all(t["nqt"] == nq and t["qc0"] == 0 for t in tiles))
        for i in range(n + LOOK):
            if i < n:
                pend.append(emit_s(tiles[i]))
            if i >= LOOK:
                emit_pv(i - LOOK, tiles[i - LOOK], pend[i - LOOK])
        while sumq:
            emit_sum(*sumq.pop(0))
        fin()

    def attnA(h):
        hb = h % 2
        for pr in range(2):
            j0 = 2 * pr
            qc = pr * 512
            q_ap = Qh[hb][:, qc:qc + 512]
            tiles = []
            for n_ in range(4 + j0 + 2):
                jj = n_ - 4 - j0
                for half_ in range(2):
                    kt = 2 * n_ + half_
                    if kt < 8:
                        k_ap, k_tl = kap_ap(h + 1, kt * 128, 128), kap_tile[h + 1]
                        v_ap, v_tl = vap_ap(h, kt * 128, 128), vap_tile[h]
                    else:
                        k_ap, k_tl = Kown[hb][:, (kt - 8) * 128:(kt - 7) * 128], t_Kown[hb]
                        v_ap, v_tl = Vown[hb][:, (kt - 8) * 128:(kt - 7) * 128], t_Vown[hb]
                    masks = [(cb_t[:, cfg.c_E + n_ * 128:cfg.c_E + (n_ + 1) * 128], MaskT[hb][:, qc:qc + 512],
                              [t_cb, t_MaskT[hb]], 0, 512)]
                    if jj in (0, 1):
                        base = jj * 256
                        if half_ == 0:
                            masks.append((IDB, TRI, [t_cb], base, 128))
                        else:
                            masks.append((IDB, NEGM, [t_cb], base, 128))
                            masks.append((IDB, TRI, [t_cb], base + 128, 128))
                    tiles.append(dict(k_ap=k_ap, k_tl=k_tl, q_ap=q_ap, q_tl=t_Qh[hb], qc0=0, nqt=512,
                                      masks=masks, v_ap=v_ap, v_tl=v_tl, on_ap=ONESB))
            bank, pv_tl = p_pv[0]
            ri = pv_i[0] % 2
            pv_i[0] += 1

            def fin(bank=bank, pv_tl=pv_tl, ri=ri, qc=qc):
                S.op(DVE, lambda e: e.tensor_copy(out=RS[0][:, 0:512], in_=bank[1][:, 0:512]), reads=[pv_tl], writes=[t_RS[0]])
                S.op(DVE, lambda e: e.tensor_copy(out=RS[1][:, 0:512], in_=bank[0][:, 0:512]), reads=[pv_tl], writes=[t_RS[1]])
                S.op(DVE, lambda e: e.reciprocal(out=RS[0][:, 0:512], in_=RS[0][:, 0:512]), reads=[t_RS[0]], writes=[t_RS[0]])
                S.op(DVE, lambda e: e.tensor_tensor(out=kap_ap(h, qc, 512), in0=RS[1][:, 0:512], in1=RS[0][:, 0:512], op=ALU.mult),
                     reads=[t_RS[0], t_RS[1]], writes=[kap_tile[h]])
            attn_tiles(tiles, bank, pv_tl, 512, SCA, fin, pair_sum=True)

    for i_ in range(2):
        S.op(DVE, lambda e, i_=i_: e.memset(MaskT[i_][:, :], 0.0), writes=[t_MaskT[i_]])
    projA_qk(0)
    ksumA(0)
    projA_v(0)
    vtrA(0)
    for h in range(AH):
        prepA1(h)
        if h + 1 < AH:
            projA_qk(h + 1)
            ksumA(h + 1)
        prepA2(h)
        if h + 1 < AH:
            projA_v(h + 1)
        attnA(h)
        if h + 1 < AH:
            vtrA(h + 1)

    if cfg.debug:
        ds_dbg = DSem(nc, "dbg")
        dbg_st = es.enter_context(nc.sbuf_tensor("dbgst", [128, 1024], F32))
        t_dbg = Tile("dbgst")

        def dump(dst, src_ap, src_tl, n):
            S.op(DVE, lambda e: e.tensor_copy(out=dbg_st[:, 0:n], in_=src_ap), reads=[src_tl], writes=[t_dbg])
            S.dma(SP, ds_dbg, [lambda e: e.dma_start(out=dst, in_=dbg_st[:, 0:n])], reads=[t_dbg])
        for h in range(AH):
            dump(dbg["dbg_ya"][h * 128:(h + 1) * 128, :], kap_ap(h), kap_tile[h], 1024)

    bar = S.barrier()
    t_tab.r = list(bar)
    load_tab(d_ropeBo)
    SCB = 64.0 ** -0.5
    CPG = cfg.CPG
    Q4 = C_t[:, 4096:4096 + CPG * 1024]
    KBv = [C_t[:, 8192 + i * 1152:8192 + (i + 1) * 1152] for i in range(2)]
    VBv = [C_t[:, 10496:11648], C_t[:, 14336:15488]]
    assert 4096 + CPG * 1024 <= 8192
    t_Q4 = Tile("q4", bar)
    t_KBv = [Tile("kbv0", bar), Tile("kbv1", bar)]
    t_VBv = [Tile("vbv0", bar), Tile("vbv1", bar)]
    t_ybt = vap_tile
    for tl in t_PT + t_RS + t_VTt + [t_RT1, t_RT2a, t_RT2b]:
        tl.r = list(bar)
    v3 = lambda ap: ap.rearrange("p (t c) -> p t c", c=128)
    for i in range(2):
        S.op(DVE, lambda e, i=i: e.memset(VBv[i], 0.0), writes=[t_VBv[i]])

    def rope_evac4(ps_ap, ps_tl, tcol, ci, tg):
        n = TG
        cos = TAB[:, tcol:tcol + n]
        ss = TAB[:, 1024 + tcol:1024 + tcol + n]
        S.op(DVE, lambda e: e.tensor_tensor(out=RT1[:, 0:n], in0=ps_ap, in1=cos, op=ALU.mult),
             reads=[ps_tl, t_tab], writes=[t_RT1])
        S.op(DVE, lambda e: e.tensor_tensor(out=RT2[0:64, 0:n], in0=ps_ap[64:128, :], in1=ss[64:128, :], op=ALU.mult),
             reads=[ps_tl, t_tab], writes=[t_RT2a])
        S.op(DVE, lambda e: e.tensor_tensor(out=RT2[64:128, 0:n], in0=ps_ap[0:64, :], in1=ss[0:64, :], op=ALU.mult),
             reads=[ps_tl, t_tab], writes=[t_RT2b])
        q4v = Q4.rearrange("p (t c q) -> p t c q", c=CPG, q=128)[:, tg * 4:(tg + 1) * 4, ci, :]
        r3 = lambda ap: ap.rearrange("p (t q) -> p t q", q=128)
        S.op(DVE, lambda e: e.tensor_tensor(out=q4v, in0=r3(RT1[:, 0:n]), in1=r3(RT2[:, 0:n]), op=ALU.add),
             reads=[t_RT1, t_RT2a, t_RT2b], writes=[t_Q4])

    def projQ4(g):
        for ci in range(CPG):
            pq = wtile("wqb", g * CPG + ci, KC)
            for tg in range(2):
                ps_ap, ps_tl = next_proj()
                mm_group(ps_ap, ps_tl, pq, xrhs(tg))
                rope_evac4(ps_ap, ps_tl, tg * TG, ci, tg)

    def prepKVB(g):
        pk = wtile("wkb", g, KC)
        for tg in range(2):
            ps_ap, ps_tl = next_proj()
            mm_group(ps_ap, ps_tl, pk, xrhs(tg))
            rope_evac(ps_ap, ps_tl, tg * TG, TG, KBv[0][:, 128 + tg * TG:128 + (tg + 1) * TG], t_KBv[0])
        S.op(DVE, lambda e: e.tensor_scalar(out=KBv[1][:, 128:1152], in0=KBv[0][:, 128:1152], scalar1=vcol(cfg.v_slot + 1), scalar2=None, op0=ALU.mult),
             reads=[t_KBv[0], t_vec], writes=[t_KBv[1]])
        S.op(DVE, lambda e: e.tensor_scalar(out=KBv[0][:, 128:1152], in0=KBv[0][:, 128:1152], scalar1=vcol(cfg.v_slot + 0), scalar2=None, op0=ALU.mult),
             reads=[t_KBv[0], t_vec], writes=[t_KBv[0]])
        for i in range(2):
            S.op(DVE, lambda e, i=i: e.tensor_scalar(out=KBv[i][:, 0:128], in0=kvb_t[:, g * 128:(g + 1) * 128], scalar1=vcol(cfg.v_slot + i), scalar2=None, op0=ALU.mult),
                 reads=[t_kvb, t_vec], writes=[t_KBv[i]])
        pv = wtile("wvb", g, KC)
        for tg in range(2):
            ps_ap, ps_tl = next_proj()
            mm_group(ps_ap, ps_tl, pv, xrhs(tg))
            v_copy(ps_ap, ps_tl, TG, tg)
        for tg in range(2):
            for j in range(4):
                S.op(PE, lambda e, j=j, tg=tg: e.transpose(out=p_vtr[0][:, j * 128:(j + 1) * 128], in_=VTt[tg][:, j * 128:(j + 1) * 128], identity=IDB),
                     reads=[t_VTt[tg], t_cb], writes=[p_vtr[1]], sig=(j == 3))
            t0 = 1 + tg * 4
            S.op(ACT, lambda e, t0=t0: e.copy(out=v3(VBv[0])[:, t0:t0 + 4, 0:64], in_=v3(p_vtr[0])[:, :, 0:64]), reads=[p_vtr[1]], writes=[t_VBv[0]])
            S.op(ACT, lambda e, t0=t0: e.copy(out=v3(VBv[1])[:, t0:t0 + 4, 64:128], in_=v3(p_vtr[0])[:, :, 64:128]), reads=[p_vtr[1]], writes=[t_VBv[1]])
        vprev = kvb_t[:, (NKV + g) * 128:(NKV + g + 1) * 128]
        S.op(ACT, lambda e: e.copy(out=VBv[0][:, 0:64], in_=vprev[:, 0:64]), reads=[t_kvb], writes=[t_VBv[0]])
        S.op(ACT, lambda e: e.copy(out=VBv[1][:, 64:128], in_=vprev[:, 64:128]), reads=[t_kvb], writes=[t_VBv[1]])

    def bcast4(ap128):
        return ap128.unsqueeze(1).broadcast_to([128, CPG, 128])

    def attnB4(g):
        c0 = g * CPG
        for qt in range(8):
            q_ap = Q4[:, qt * CPG * 128:(qt + 1) * CPG * 128]
            tiles = []
            for i in range(2):
                for which in range(2):
                    ktl = qt + which
                    msk = TRI if which == 1 else (TRI2F if qt == 0 else TRI2)
                    masks = [(IDB, bcast4(msk), [t_cb], 0, CPG * 128)]
                    tiles.append(dict(k_ap=KBv[i][:, ktl * 128:(ktl + 1) * 128], k_tl=t_KBv[i], q_ap=q_ap, q_tl=t_Q4, qc0=0, nqt=CPG * 128,
                                      masks=masks,
                                      v_ap=VBv[i][:, ktl * 128:(ktl + 1) * 128], v_tl=t_VBv[i], on_ap=ONI[i]))
            bank, pv_tl = p_pv[0]
            NQ = CPG * 128

            def fin(bank=bank, pv_tl=pv_tl, qt=qt):
                S.op(DVE, lambda e: e.tensor_copy(out=RS[0][:, 0:NQ], in_=bank[1][:, 0:NQ]), reads=[pv_tl], writes=[t_RS[0]])
                S.op(DVE, lambda e: e.tensor_copy(out=RS[1][:, 0:NQ], in_=bank[0][:, 0:NQ]), reads=[pv_tl], writes=[t_RS[1]])
                for ci in range(CPG):
                    S.op(DVE, lambda e, ci=ci: e.tensor_scalar(out=RS[0][:, ci * 128:(ci + 1) * 128], in0=RS[0][:, ci * 128:(ci + 1) * 128],
                                                               scalar1=vec_t[:, v_esk + c0 + ci:v_esk + c0 + ci + 1], scalar2=None, op0=ALU.add),
                         reads=[t_RS[0], t_vec2], writes=[t_RS[0]])
                S.op(DVE, lambda e: e.reciprocal(out=RS[0][:, 0:NQ], in_=RS[0][:, 0:NQ]), reads=[t_RS[0]], writes=[t_RS[0]])
                yv = B_t[:, VAP0 + c0 * 1024:VAP0 + (c0 + CPG) * 1024].rearrange("p (c t) -> p c t", t=1024)[:, :, qt * 128:(qt + 1) * 128]
                r3 = lambda ap: ap.rearrange("p (c q) -> p c q", q=128)
                S.op(DVE, lambda e: e.tensor_tensor(out=yv, in0=r3(RS[1][:, 0:NQ]), in1=r3(RS[0][:, 0:NQ]), op=ALU.mult),
                     reads=[t_RS[0], t_RS[1]], writes=[t_ybt[c0 + ci] for ci in range(CPG)])
            attn_tiles(tiles, bank, pv_tl, NQ, SCB, fin)

    for g in range(NKV):
        projQ4(g)
        prepKVB(g)
        attnB4(g)

    if cfg.debug:
        for c in range(BC):
            dump(dbg["dbg_yb"][c * 128:(c + 1) * 128, :], vap_ap(c), vap_tile[c], 1024)

    bar = S.barrier()
    pb = [Tile(f"pb{i}", bar) for i in range(8)]
    c_tile = [Tile(f"c{i}", bar) for i in range(32)]
    p2 = [Tile(f"p2_{i}", bar) for i in range(6)]

    def c_ap(i, n=512):
        return C_t[:, i * 512:i * 512 + n]

    def p2_ap(i):
        return f32v(P_t[:, i * 1024:(i + 1) * 1024])

    S1A, S2A = p2_ap(2), p2_ap(3)
    t_s1a, t_s2a = p2[2], p2[3]

    def ya_rhs(tg):
        return lambda k: (kap_ap(k, tg * TG, TG), kap_tile[k])

    def yb_rhs(tg):
        return lambda k: (vap_ap(k, tg * TG, TG), vap_tile[k])

    def mg_ap(tg, c):
        return (c_ap(c), c_tile[c]) if tg == 0 else (a_ap(0, c), xt_tile[0][c // 4])

    for tg in range(2):
        for c in range(DC):
            b0 = 4 * (c % 2)
            pga = wtile("wga", c, KC)
            mm_group(banks[b0][:, :], pb[b0], pga, xrhs(tg))
            pgb = wtile("wgb", c, KC)
            mm_group(banks[b0 + 1][:, :], pb[b0 + 1], pgb, xrhs(tg))
            pua = wtile("wupa", c, AH)
            mm_group(banks[b0 + 2][:, :], pb[b0 + 2], pua, ya_rhs(tg))
            pub = wtile("wupb", c, BC)
            mm_group(banks[b0 + 3][:, :], pb[b0 + 3], pub, yb_rhs(tg))
            S.op(ACT, lambda e, b0=b0, c=c: e.activation(out=p2_ap(0), in_=banks[b0][:, :], func=AF.Sigmoid, bias=vcol(cfg.v_bga + c), scale=1.0),
                 reads=[pb[b0], t_vec], writes=[p2[0]])
            S.op(ACT, lambda e, b0=b0, c=c: e.activation(out=p2_ap(1), in_=banks[b0 + 1][:, :], func=AF.Sigmoid, bias=vcol(cfg.v_bgb + c), scale=1.0),
                 reads=[pb[b0 + 1], t_vec], writes=[p2[1]])
            S.op(DVE, lambda e, b0=b0: e.tensor_tensor(out=p2_ap(2), in0=banks[b0 + 2][:, :], in1=p2_ap(0), op=ALU.mult),
                 reads=[pb[b0 + 2], p2[0]], writes=[p2[2]])
            S.op(DVE, lambda e, b0=b0: e.tensor_tensor(out=p2_ap(3), in0=banks[b0 + 3][:, :], in1=p2_ap(1), op=ALU.mult),
                 reads=[pb[b0 + 3], p2[1]], writes=[p2[3]])
            m_ap, m_tl = mg_ap(tg, c)
            S.op(DVE, lambda e, m_ap=m_ap: e.tensor_tensor(out=m_ap, in0=p2_ap(2), in1=p2_ap(3), op=ALU.add),
                 reads=[p2[2], p2[3]], writes=[m_tl])
            if cfg.debug:
                dump(dbg["dbg_mg"][c * 128:(c + 1) * 128, tg * TG:(tg + 1) * TG], m_ap, m_tl, TG)

    bar = S.barrier()
    bq = [Tile(f"bq{i}", bar) for i in range(DC)]
    a2_tile = [Tile(f"a2_{i}", bar) for i in range(DC)]

    def bq_ap(oc):
        return f32v(B_t[:, oc * 1024:(oc + 1) * 1024])

    ds_x32 = [DSem(nc, "x32_0"), DSem(nc, "x32_1")]
    ds_ple = [DSem(nc, "ple0"), DSem(nc, "ple1")]
    t_ple = [Tile("ple0"), Tile("ple1")]
    ple_i = [0]
    ds_out = [DSem(nc, "out0"), DSem(nc, "out1")]

    def ln_finish(stat_s1, stat_s2, gcol, bcol, bcol2, dbg_name, tg, scale_out):
        MEAN, RSTD, TMP = p2_ap(2), p2_ap(3), p2_ap(4)
        S.op(DVE, lambda e: e.tensor_scalar(out=MEAN, in0=banks[stat_s1][:, :], scalar1=1.0 / D, scalar2=None, op0=ALU.mult),
             reads=[pb[stat_s1]], writes=[p2[2]])
        S.op(DVE, lambda e: e.tensor_tensor(out=TMP, in0=MEAN, in1=MEAN, op=ALU.mult), reads=[p2[2]], writes=[p2[4]])
        S.op(DVE, lambda e: e.scalar_tensor_tensor(out=RSTD, in0=banks[stat_s2][:, :], scalar=1.0 / D, in1=TMP, op0=ALU.mult, op1=ALU.subtract),
             reads=[pb[stat_s2], p2[4]], writes=[p2[3]])
        S.op(ACT, lambda e: e.activation(out=RSTD, in_=RSTD, func=AF.Sqrt, bias=vcol(cfg.v_eps), scale=1.0),
             reads=[p2[3], t_vec], writes=[p2[3]])
        S.op(DVE, lambda e: e.reciprocal(out=RSTD, in_=RSTD), reads=[p2[3]], writes=[p2[3]])
        S.op(DVE, lambda e: e.scalar_tensor_tensor(out=MEAN, in0=MEAN, scalar=-1.0, in1=RSTD, op0=ALU.mult, op1=ALU.mult),
             reads=[p2[2], p2[3]], writes=[p2[2]])
        for oc in range(DC):
            ti = 4 + (oc % 2)
            gc = vec_t[:, gcol + oc:gcol + oc + 1]
            S.op(DVE, lambda e, oc=oc, ti=ti, gc=gc: e.scalar_tensor_tensor(out=p2_ap(ti), in0=bq_ap(oc), scalar=gc, in1=RSTD, op0=ALU.mult, op1=ALU.mult),
                 reads=[bq[oc], p2[3], t_vec, t_vec2], writes=[p2[ti]])
            S.op(DVE, lambda e, oc=oc, ti=ti, gc=gc: e.scalar_tensor_tensor(out=p2_ap(ti), in0=MEAN, scalar=gc, in1=p2_ap(ti), op0=ALU.mult, op1=ALU.add),
                 reads=[p2[2], p2[ti], t_vec, t_vec2], writes=[p2[ti]])
            S.op(ACT, lambda e, oc=oc, ti=ti: e.activation(out=a_ap(1, oc), in_=p2_ap(ti), func=AF.Identity, bias=vec_t[:, bcol2 + oc:bcol2 + oc + 1], scale=float(scale_out)),
                 reads=[p2[ti], t_vec], writes=[a2_tile[oc]])
            S.op(ACT, lambda e, oc=oc, ti=ti: e.activation(out=bq_ap(oc), in_=p2_ap(ti), func=AF.Identity, bias=vec_t[:, bcol + oc:bcol + oc + 1], scale=1.0),
                 reads=[p2[ti], t_vec, t_vec2], writes=[bq[oc]])
            if cfg.debug:
                S.op(ACT, lambda e, oc=oc: e.activation(out=dbg_st[:, 0:TG], in_=bq_ap(oc), func=AF.Identity, scale=float(scale_out)),
                     reads=[bq[oc]], writes=[t_dbg])
                S.dma(SP, ds_dbg, [lambda e, oc=oc: e.dma_start(out=dbg[dbg_name][oc * 128:(oc + 1) * 128, tg * TG:(tg + 1) * TG], in_=dbg_st[:, 0:TG])], reads=[t_dbg])

    def stats_acc(oc, sq_i):
        S.op(ACT, lambda e: e.activation(out=p2_ap(sq_i), in_=bq_ap(oc), func=AF.Square), reads=[bq[oc]], writes=[p2[sq_i]])
        if oc == 0:
            S.op(DVE, lambda e: e.tensor_copy(out=S1A, in_=bq_ap(oc)), reads=[bq[oc]], writes=[t_s1a])
            S.op(DVE, lambda e: e.tensor_copy(out=S2A, in_=p2_ap(sq_i)), reads=[p2[sq_i]], writes=[t_s2a])
        else:
            S.op(DVE, lambda e: e.tensor_tensor(out=S1A, in0=S1A, in1=bq_ap(oc), op=ALU.add), reads=[bq[oc], t_s1a], writes=[t_s1a])
            S.op(DVE, lambda e: e.tensor_tensor(out=S2A, in0=S2A, in1=p2_ap(sq_i), op=ALU.add), reads=[p2[sq_i], t_s2a], writes=[t_s2a])

    def stats_mm(s1, s2):
        S.op(PE, lambda e: e.matmul(banks[s1][:, :], lhsT=one32_t[:, :], rhs=S1A, start=True, stop=True),
             reads=[t_one32, t_s1a], writes=[pb[s1]], sig=True)
        S.op(PE, lambda e: e.matmul(banks[s2][:, :], lhsT=one32_t[:, :], rhs=S2A, start=True, stop=True),
             reads=[t_one32, t_s2a], writes=[pb[s2]], sig=True)

    for tg in range(2):
        def mrhs(k, tg=tg):
            return mg_ap(tg, k)
        for oc in range(DC):
            pw = wtile("wo", oc, DC)
            b = oc % 2
            mm_group(banks[b][:, :], pb[b], pw, mrhs)
            xi = oc % 2
            S.dma(SP, ds_x32[xi], [lambda e, oc=oc, xi=xi: e.dma_start(out=p2_ap(xi), in_=d_xo[oc * 128:(oc + 1) * 128, tg * TG:(tg + 1) * TG])],
                  writes=[p2[xi]])
            S.op(DVE, lambda e, oc=oc, xi=xi, b=b: e.scalar_tensor_tensor(out=bq_ap(oc), in0=p2_ap(xi), scalar=float(cfg.alpha), in1=banks[b][:, :],
                                                                     op0=ALU.mult, op1=ALU.add),
                 reads=[p2[xi], pb[b]], writes=[bq[oc]])
            stats_acc(oc, 4 + (oc % 2))
        stats_mm(2, 3)
        ln_finish(2, 3, v_ag1, v_ag1 + DC, cfg.v_l1b, "dbg_h1", tg, 1.0 / cfg.alpha)

        def hrhs(k):
            return (a_ap(1, k), a2_tile[k])
        ub_i = [0]

        def ffn_up(fg):
            for j in range(cfg.FGS):
                fc = fg * cfg.FGS + j
                pw = wtile("wffu", fc, KC)
                b = ub_i[0] % 2
                ub_i[0] += 1
                mm_group(banks[b][:, :], pb[b], pw, hrhs)
                ui = (fg % 2) * cfg.FGS + j
                S.op(ACT, lambda e, b=b: e.activation(out=p2_ap(b), in_=banks[b][:, :], func=AF.Relu), reads=[pb[b]], writes=[p2[b]])
                S.op(DVE, lambda e, b=b, ui=ui: e.tensor_tensor(out=c_ap(ui), in0=p2_ap(b), in1=p2_ap(b), op=ALU.mult),
                     reads=[p2[b]], writes=[c_tile[ui]])

        db_i = [0]

        def ffn_down(fg):
            for ocp in range(DC // 2):
                ap, tl = wload(d_w["wffd"][fg * (DC // 2) + ocp, :, :], 2 * cfg.FGS * 128)
                for oi in range(2):
                    oc = ocp * 2 + oi
                    b = 4 + db_i[0] % 4
                    db_i[0] += 1
                    for j in range(cfg.FGS):
                        ui = (fg % 2) * cfg.FGS + j
                        lhs = ap[:, (oi * cfg.FGS + j) * 128:(oi * cfg.FGS + j + 1) * 128]
                        S.op(PE, lambda e, lhs=lhs, ui=ui, b=b, j=j: e.matmul(banks[b][:, :], lhsT=lhs, rhs=c_ap(ui), start=(j == 0), stop=(j == cfg.FGS - 1)),
                             reads=[tl, c_tile[ui]], writes=[pb[b]], sig=(j == cfg.FGS - 1))
                    S.op(DVE, lambda e, oc=oc, b=b: e.tensor_tensor(out=bq_ap(oc), in0=banks[b][:, :], in1=bq_ap(oc), op=ALU.add),
                         reads=[pb[b], bq[oc]], writes=[bq[oc]])
                    if fg == cfg.FG - 1:
                        stats_acc(oc, 4 + (oc % 2))

        ffn_up(0)
        for fg in range(cfg.FG):
            if fg + 1 < cfg.FG:
                ffn_up(fg + 1)
            ffn_down(fg)
        stats_mm(2, 3)
        ln_finish(2, 3, cfg.v_l2g, cfg.v_l2b, cfg.v_l2b, "dbg_h2", tg, 1.0)

        for oc in range(DC):
            if oc % cfg.NPO == 0:
                pi_ = ple_i[0] % 2
                ple_i[0] += 1
                pl_ap, pl_tl = ple_t[:, pi_ * cfg.NPO * 256:(pi_ + 1) * cfg.NPO * 256], t_ple[pi_]
                S.dma(POOL, ds_ple[pi_], [lambda e, oc=oc, pl_ap=pl_ap: e.dma_start(out=pl_ap, in_=d_w["wple"][oc // cfg.NPO, :, :])], writes=[pl_tl])
            pw = wtile("wpg", oc, KC)
            b = oc % 2
            mm_group(banks[b][:, :], pb[b], pw, hrhs)
            b2 = 4 + oc % 2
            o_ = oc % cfg.NPO
            for kc2 in range(2):
                lhs = pl_ap[:, (o_ * 2 + kc2) * 128:(o_ * 2 + kc2 + 1) * 128]
                S.op(PE, lambda e, lhs=lhs, kc2=kc2, b2=b2: e.matmul(banks[b2][:, :], lhsT=lhs, rhs=pt_t[:, kc2 * T + tg * TG:kc2 * T + (tg + 1) * TG], start=(kc2 == 0), stop=(kc2 == 1)),
                     reads=[pl_tl, t_pt], writes=[pb[b2]], sig=(kc2 == 1))
            si = oc % 2
            S.op(ACT, lambda e, b=b, si=si: e.activation(out=p2_ap(si), in_=banks[b][:, :], func=AF.Sigmoid), reads=[pb[b]], writes=[p2[si]])
            S.op(DVE, lambda e, b2=b2, si=si: e.tensor_tensor(out=p2_ap(si), in0=banks[b2][:, :], in1=p2_ap(si), op=ALU.mult),
                 reads=[pb[b2], p2[si]], writes=[p2[si]])
            oi_ = 2 + oc % 2
            S.op(DVE, lambda e, oc=oc, si=si, oi_=oi_: e.tensor_tensor(out=p2_ap(oi_), in0=p2_ap(si), in1=bq_ap(oc), op=ALU.add),
                 reads=[p2[si], bq[oc]], writes=[p2[oi_]])
            S.dma(SP, ds_out[oc % 2], [lambda e, oc=oc, oi_=oi_: e.dma_start(out=d_out[oc * 128:(oc + 1) * 128, tg * TG:(tg + 1) * TG], in_=p2_ap(oi_))],
                  reads=[p2[oi_]])

    for ds in ds_out:
        SP.e.wait_ge(ds.sem, ds.val)
    if cfg.debug:
        SP.e.wait_ge(ds_dbg.sem, ds_dbg.val)
    es.close()
    return nc


_CACHE = {}


def kernel(**inputs):
    cfg = Cfg(D=4096)
    sh = prep_shared(cfg, inputs)
    in_maps = [prep_core(cfg, inputs, sh, c) for c in range(8)]
    if "nc" not in _CACHE:
        _CACHE["nc"] = build(cfg)
    res = run_bass_kernel_spmd(_CACHE["nc"], in_maps, core_ids=list(range(8)))
    B, SEQ, T = 4, cfg.SEQ, cfg.T
    out = np.empty((B, SEQ, cfg.D), np.float32)
    for c in range(8):
        b, half = c // 2, c % 2
        out[b, half * T:(half + 1) * T, :] = res.results[c]["out_t"].T
    return out
```
